# Optimizing a Trainium2 kernel written in Bass

```python
import jax, jax.numpy as jnp
from jax import lax
import numpy as np

D_MODEL = 2048
BATCH = 2
SEQ = 16384
DEPTH = 2

HEAD_DIM = 128
D_FF = 5632
PLE_DIM = 256
NORM_EPS = 1e-6
NEG_INF = -1e30
POOL_WINDOWS = (2, 4, 8, 16)
POOL_GROUPS = 4
POOL_WIDTH = 1024
POOL_GROUP_WIDTH = POOL_WIDTH // POOL_GROUPS
ATTN_PATTERNS = ((128, 1), (512, 4), (2048, 16))
ATTN_HEADS_PER_GROUP = 4
ATTN_HEADS = ATTN_HEADS_PER_GROUP * len(ATTN_PATTERNS)
ATTN_WIDTH = ATTN_HEADS * HEAD_DIM
REL_BUCKETS = 32
REL_MAX_DISTANCE = 1024
RET_HEADS = 4
RET_QK_DIM = 128
RET_V_DIM = 256
RET_QK_WIDTH = RET_HEADS * RET_QK_DIM
RET_V_WIDTH = RET_HEADS * RET_V_DIM
RET_CHUNK = 128
ROPE_BASE = 10000.0
CONV_WIDTH = 1024
CONV_SIZE = 31
N_BRANCHES = 4
IN_SPLITS = (POOL_WIDTH, ATTN_WIDTH, ATTN_WIDTH, ATTN_WIDTH,
             RET_QK_WIDTH, RET_QK_WIDTH, RET_V_WIDTH, RET_V_WIDTH, 2 * CONV_WIDTH)
IN_WIDTH = POOL_WIDTH + 3 * ATTN_WIDTH + 2 * RET_QK_WIDTH + 2 * RET_V_WIDTH + 2 * CONV_WIDTH

kernel_name = "hybrid_parallel_encoder_block"

F32 = jnp.float32


def _rms_norm(x, gain):
    xf = x.astype(F32)
    xf = xf * lax.rsqrt(jnp.mean(xf * xf, axis=-1, keepdims=True) + NORM_EPS)
    return (xf * gain.astype(F32)).astype(x.dtype)


def _layer_norm(x, gain, bias):
    xf = x.astype(F32)
    mu = jnp.mean(xf, axis=-1, keepdims=True)
    var = jnp.mean(jnp.square(xf - mu), axis=-1, keepdims=True)
    return ((xf - mu) * lax.rsqrt(var + NORM_EPS) * gain.astype(F32) + bias.astype(F32)).astype(x.dtype)


def _swiglu(x, w_gate, w_up, w_down):
    return (jax.nn.silu(x @ w_gate) * (x @ w_up)) @ w_down


def _pool_mixer(xp, pool_w, pool_scale):
    B, S, _ = xp.shape
    xg = xp.reshape(B, S, POOL_GROUPS, POOL_GROUP_WIDTH).astype(F32)
    cs = jnp.concatenate([jnp.zeros_like(xg[:, :1]), jnp.cumsum(xg, axis=1)], axis=1)
    t = np.arange(S)[:, None]
    w = np.array(POOL_WINDOWS)[None, :]
    lo = np.clip(t - w // 2, 0, S)
    hi = np.clip(t - w // 2 + w, 0, S)
    cnt = (hi - lo).astype(np.float32)
    g = np.arange(POOL_GROUPS)[None, :]
    mean = (cs[:, hi, g] - cs[:, lo, g]) / cnt[None, :, :, None]
    mixed = (mean - xg).astype(xp.dtype)
    y = jnp.einsum('bsgc,gcd->bsgd', mixed, pool_w).reshape(B, S, POOL_WIDTH)
    return y * pool_scale


def _t5_bucket(rel):
    half = REL_BUCKETS // 2
    exact = half // 2
    offset = np.where(rel > 0, half, 0)
    n = np.abs(rel)
    large = exact + (np.log(np.maximum(n, 1) / exact) / np.log(REL_MAX_DISTANCE / exact)
                     * (half - exact)).astype(np.int32)
    large = np.minimum(large, half - 1)
    return (offset + np.where(n < exact, n, large)).astype(np.int32)


def _dilated_group(q, k, v, bias_table, window, dilation):
    B, S, Hg, Dh = q.shape
    radius = window // (2 * dilation)
    blk = radius
    L = S // dilation
    nblk = -(-L // blk)
    Lp = nblk * blk

    def to_sub(t):
        return t.reshape(B, L, dilation, Hg, Dh).transpose(0, 2, 1, 3, 4)

    def from_sub(t):
        t = t.reshape((B, dilation, Lp) + t.shape[4:])[:, :, :L]
        return jnp.swapaxes(t, 1, 2).reshape((B, S) + t.shape[3:])

    qs, ks, vs = to_sub(q), to_sub(k), to_sub(v)
    qb = jnp.pad(qs, ((0, 0), (0, 0), (0, Lp - L), (0, 0), (0, 0))).reshape(B, dilation, nblk, blk, Hg, Dh)
    pad = ((0, 0), (0, 0), (blk, Lp - L + blk), (0, 0), (0, 0))

    def band_windows(t):
        tb = jnp.pad(t, pad).reshape(B, dilation, nblk + 2, blk, Hg, Dh)
        return jnp.concatenate([tb[:, :, :-2], tb[:, :, 1:-1], tb[:, :, 2:]], axis=3)

    kw, vw = band_windows(ks), band_windows(vs)
    a = np.arange(blk)[:, None]
    b = np.arange(3 * blk)[None, :]
    rel_sub = b - blk - a
    band = np.abs(rel_sub) <= radius
    kpos = np.arange(nblk)[:, None] * blk - blk + np.arange(3 * blk)[None, :]
    valid = (kpos >= 0) & (kpos < L)
    mask = band[None] & valid[:, None, :]
    bias = jnp.take(bias_table, _t5_bucket(rel_sub * dilation), axis=0)
    bias = bias.transpose(2, 0, 1).astype(F32)

    logits = jnp.einsum('bdnqhc,bdnkhc->bdnhqk', qb, kw) + bias[None, None, None]
    logits = jnp.where(mask[None, None, :, None], logits, NEG_INF)
    m = jnp.max(logits, axis=-1, keepdims=True)
    e = jnp.exp(logits - m)
    s = jnp.sum(e, axis=-1, keepdims=True)
    o = jnp.einsum('bdnhqk,bdnkhc->bdnqhc', e, vw) / s.transpose(0, 1, 2, 4, 3, 5)
    lse = (m + jnp.log(s))[..., 0].transpose(0, 1, 2, 4, 3)
    return from_sub(o), from_sub(lse)


def _dilated_attention(q, k, v, q_norm, k_norm, rel_bias):
    B, S = q.shape[:2]
    dtype = q.dtype
    qn = _rms_norm(q, q_norm).astype(F32) * (HEAD_DIM ** -0.5)
    kn = _rms_norm(k, k_norm).astype(F32)
    vf = v.astype(F32)
    outs, lses = [], []
    for g, (window, dilation) in enumerate(ATTN_PATTERNS):
        hs = slice(g * ATTN_HEADS_PER_GROUP, (g + 1) * ATTN_HEADS_PER_GROUP)
        o, lse = _dilated_group(qn[:, :, hs], kn[:, :, hs], vf[:, :, hs], rel_bias[:, hs], window, dilation)
        outs.append(o)
        lses.append(lse)
    alpha = jax.nn.softmax(jnp.stack(lses, axis=0), axis=0)
    y = jnp.concatenate([outs[g] * alpha[g][..., None] for g in range(len(ATTN_PATTERNS))], axis=2)
    return y.reshape(B, S, ATTN_WIDTH).astype(dtype)


def _rotary(x, pos):
    half = x.shape[-1] // 2
    inv = ROPE_BASE ** (-jnp.linspace(0.0, 1.0, half, dtype=F32))
    ang = pos[:, None] * inv[None, :]
    cos = jnp.cos(ang)[None, :, None, :]
    sin = jnp.sin(ang)[None, :, None, :]
    x1, x2 = x[..., :half], x[..., half:]
    return jnp.concatenate([x1 * cos - x2 * sin, x1 * sin + x2 * cos], axis=-1)


def _retention_one_direction(q, k, v, log_gamma, include_diag):
    B, S, H, dk = q.shape
    dv = v.shape[-1]
    C = RET_CHUNK
    N = S // C

    def chunks(t):
        return t.reshape(B, N, C, H, t.shape[-1]).transpose(1, 0, 3, 2, 4)

    idx = np.arange(C, dtype=np.float32)
    diff = idx[:, None] - idx[None, :]
    tri = (diff >= 0) if include_diag else (diff > 0)
    decay_intra = jnp.where(tri[None], jnp.exp(np.where(tri, diff, 0.0)[None] * log_gamma[:, None, None]), 0.0)
    xi = jnp.exp((idx + 1.0)[None, :] * log_gamma[:, None])
    zeta = jnp.exp((C - 1.0 - idx)[None, :] * log_gamma[:, None])
    g_chunk = jnp.exp(C * log_gamma)

    def step(state, qkv):
        qc, kc, vc = qkv
        scores = jnp.einsum('bhqc,bhkc->bhqk', qc, kc) * decay_intra
        inner = jnp.einsum('bhqk,bhkv->bhqv', scores, vc)
        cross = jnp.einsum('bhqc,bhcv->bhqv', qc * xi[..., None], state)
        state = state * g_chunk[:, None, None] + jnp.einsum('bhkc,bhkv->bhcv', kc * zeta[..., None], vc)
        return state, inner + cross

    state0 = jnp.zeros((B, H, dk, dv), F32)
    _, out = lax.scan(step, state0, (chunks(q), chunks(k), chunks(v)))
    return out.transpose(1, 0, 3, 2, 4).reshape(B, S, H, dv)


def _retention(q, k, v, gate, decay_logit, ret_norm):
    B, S, _ = q.shape
    dtype = q.dtype
    pos = jnp.arange(S, dtype=F32)
    qr = _rotary(q.reshape(B, S, RET_HEADS, RET_QK_DIM).astype(F32), pos)
    kr = _rotary(k.reshape(B, S, RET_HEADS, RET_QK_DIM).astype(F32), pos) * (RET_QK_DIM ** -0.5)
    vr = v.reshape(B, S, RET_HEADS, RET_V_DIM).astype(F32)
    log_gamma = jax.nn.log_sigmoid(decay_logit.astype(F32))
    fwd = _retention_one_direction(qr, kr, vr, log_gamma[0], True)
    bwd = _retention_one_direction(qr[:, ::-1], kr[:, ::-1], vr[:, ::-1], log_gamma[1], False)[:, ::-1]
    y = fwd + bwd
    y = y * lax.rsqrt(jnp.mean(y * y, axis=-1, keepdims=True) + NORM_EPS)
    y = y * ret_norm.reshape(RET_HEADS, RET_V_DIM).astype(F32)
    return y.reshape(B, S, RET_V_WIDTH).astype(dtype) * jax.nn.silu(gate)


def _conv_module(c_in, conv_w, conv_b, norm_g, norm_b):
    a, g = jnp.split(c_in, 2, axis=-1)
    h = a * jax.nn.sigmoid(g)
    h = lax.conv_general_dilated(h, conv_w[:, None, :].astype(h.dtype), window_strides=(1,),
                                 padding=[(CONV_SIZE // 2, CONV_SIZE // 2)],
                                 dimension_numbers=('NWC', 'WIO', 'NWC'),
                                 feature_group_count=CONV_WIDTH) + conv_b
    return jax.nn.silu(_layer_norm(h, norm_g, norm_b))


def setup_inputs(seed: int = 0) -> dict:
    key = jax.random.key(seed)
    ks = iter(jax.random.split(key, 48))
    L = DEPTH

    def w(shape, fan_in):
        return jax.random.normal(next(ks), shape, F32) * (fan_in ** -0.5)

    def gain(shape):
        return 1.0 + 0.05 * jax.random.normal(next(ks), shape, F32)

    def small(shape, s=0.02):
        return s * jax.random.normal(next(ks), shape, F32)

    decay_base = jnp.asarray(np.log(2.0 ** (5 + np.arange(RET_HEADS)) - 1.0).astype(np.float32))
    return {
        "x": jax.random.normal(next(ks), (BATCH, SEQ, D_MODEL), F32),
        "p": jax.random.normal(next(ks), (DEPTH, BATCH, SEQ, PLE_DIM), F32),
        "rel_bias": small((REL_BUCKETS, ATTN_HEADS), 0.3),
        "ffn1_norm": gain((L, D_MODEL)),
        "ffn1_w_gate": w((L, D_MODEL, D_FF), D_MODEL),
        "ffn1_w_up": w((L, D_MODEL, D_FF), D_MODEL),
        "ffn1_w_down": w((L, D_FF, D_MODEL), D_FF),
        "mix_norm": gain((L, D_MODEL)),
        "w_in": w((L, D_MODEL, IN_WIDTH), D_MODEL),
        "pool_w": w((L, POOL_GROUPS, POOL_GROUP_WIDTH, POOL_GROUP_WIDTH), POOL_GROUP_WIDTH),
        "pool_scale": gain((L, POOL_WIDTH)),
        "q_norm": gain((L, HEAD_DIM)),
        "k_norm": gain((L, HEAD_DIM)),
        "ret_decay_logit": decay_base[None, None, :] + small((L, 2, RET_HEADS), 0.1),
        "ret_norm": gain((L, RET_V_WIDTH)),
        "conv_w": w((L, CONV_SIZE, CONV_WIDTH), CONV_SIZE),
        "conv_b": small((L, CONV_WIDTH)),
        "conv_norm_g": gain((L, CONV_WIDTH)),
        "conv_norm_b": small((L, CONV_WIDTH)),
        "w_gate": w((L, D_MODEL, N_BRANCHES * D_MODEL), D_MODEL),
        "b_gate": small((L, N_BRANCHES * D_MODEL)),
        "w_br_pool": w((L, POOL_WIDTH, D_MODEL), POOL_WIDTH),
        "w_br_attn": w((L, ATTN_WIDTH, D_MODEL), ATTN_WIDTH),
        "w_br_ret": w((L, RET_V_WIDTH, D_MODEL), RET_V_WIDTH),
        "w_br_conv": w((L, CONV_WIDTH, D_MODEL), CONV_WIDTH),
        "w_out": w((L, D_MODEL, D_MODEL), D_MODEL),
        "ffn2_norm": gain((L, D_MODEL)),
        "ffn2_w_gate": w((L, D_MODEL, D_FF), D_MODEL),
        "ffn2_w_up": w((L, D_MODEL, D_FF), D_MODEL),
        "ffn2_w_down": w((L, D_FF, D_MODEL), D_FF),
        "ple_norm": gain((L, D_MODEL)),
        "w_ple_gate": w((L, D_MODEL, D_MODEL), D_MODEL),
        "w_ple_proj": w((L, PLE_DIM, D_MODEL), PLE_DIM),
    }


def reference(x, p, rel_bias, ffn1_norm, ffn1_w_gate, ffn1_w_up, ffn1_w_down, mix_norm, w_in,
              pool_w, pool_scale, q_norm, k_norm, ret_decay_logit, ret_norm, conv_w, conv_b,
              conv_norm_g, conv_norm_b, w_gate, b_gate, w_br_pool, w_br_attn, w_br_ret, w_br_conv,
              w_out, ffn2_norm, ffn2_w_gate, ffn2_w_up, ffn2_w_down, ple_norm, w_ple_gate, w_ple_proj):
    B, S, _ = x.shape
    split_at = np.cumsum(IN_SPLITS)[:-1]
    h = x
    for i in range(DEPTH):
        h = h + 0.5 * _swiglu(_rms_norm(h, ffn1_norm[i]), ffn1_w_gate[i], ffn1_w_up[i], ffn1_w_down[i])
        u = _rms_norm(h, mix_norm[i])
        xp, aq, ak, av, rq, rk, rv, rg, cin = jnp.split(u @ w_in[i], split_at, axis=-1)
        y_pool = _pool_mixer(xp, pool_w[i], pool_scale[i])
        y_attn = _dilated_attention(aq.reshape(B, S, ATTN_HEADS, HEAD_DIM),
                                    ak.reshape(B, S, ATTN_HEADS, HEAD_DIM),
                                    av.reshape(B, S, ATTN_HEADS, HEAD_DIM),
                                    q_norm[i], k_norm[i], rel_bias)
        y_ret = _retention(rq, rk, rv, rg, ret_decay_logit[i], ret_norm[i])
        y_conv = _conv_module(cin, conv_w[i], conv_b[i], conv_norm_g[i], conv_norm_b[i])
        gates = jax.nn.sigmoid(u @ w_gate[i] + b_gate[i]).reshape(B, S, N_BRANCHES, D_MODEL)
        merged = (gates[:, :, 0] * (y_pool @ w_br_pool[i])
                  + gates[:, :, 1] * (y_attn @ w_br_attn[i])
                  + gates[:, :, 2] * (y_ret @ w_br_ret[i])
                  + gates[:, :, 3] * (y_conv @ w_br_conv[i]))
        h = h + merged @ w_out[i]
        h = h + 0.5 * _swiglu(_rms_norm(h, ffn2_norm[i]), ffn2_w_gate[i], ffn2_w_up[i], ffn2_w_down[i])
        ple = p[i] @ w_ple_proj[i]
        h = h + jax.nn.sigmoid(_rms_norm(h, ple_norm[i]) @ w_ple_gate[i]) * ple
    return h
```

```python
import numpy as np
import ml_dtypes
from contextlib import ExitStack
import concourse.bass as bass
import concourse.mybir as mybir
from concourse.bass_utils import run_bass_kernel_spmd

F32 = mybir.dt.float32
BF16 = mybir.dt.bfloat16
ALU = mybir.AluOpType
AF = mybir.ActivationFunctionType
AX = mybir.AxisListType

NCORES = 8
D = 2048
DFF = 5632
KC = D // 128
FC = DFF // 128
NT = 512
TPC = 4096
SEQ = 16384
DEPTH = 2
EPS = 1e-6
IN_W = 10752
CH_XP = (0, 8); CH_AQ = (8, 20); CH_AK = (20, 32); CH_AV = (32, 44)
CH_RQ = (44, 48); CH_RK = (48, 52); CH_RV = (52, 60); CH_RG = (60, 68); CH_CIN = (68, 84)
EPOCH = 11000


_UID = [0]


def uname(n):
    _UID[0] += 1
    return f"{n}_u{_UID[0]}"


class T:
    __slots__ = ("ap", "lw", "rd", "name")

    def __init__(self, ap, name=""):
        self.ap = ap
        self.lw = None
        self.rd = {}
        self.name = name


class Eng:
    def __init__(self, kb, eng, name):
        self.kb = kb
        self.eng = eng
        self.name = name
        self.sem = kb.new_sem(name + "_s0")
        self.count = 0
        self.nsem = 1
        self.waited = {}
        self.pend_r = []
        self.pend_w = []

    def wait(self, sem, val):
        if self.waited.get(sem, 0) >= val:
            return
        self.eng.wait_ge(sem, val)
        self.waited[sem] = val

    def bump(self, inst):
        if self.count >= EPOCH:
            self.sem = self.kb.new_sem(f"{self.name}_s{self.nsem}")
            self.nsem += 1
            self.count = 0
        self.count += 1
        inst.then_inc(self.sem, 1)
        return (self.sem, self.count)


class KB:
    def __init__(self, nc, es, n_dma_sems=24):
        self.nc = nc
        self.es = es
        self.root_es = es
        self._semn = 0
        self.E = {
            "pe": Eng(self, nc.tensor, "pe"),
            "dve": Eng(self, nc.vector, "dve"),
            "act": Eng(self, nc.scalar, "act"),
            "pool": Eng(self, nc.gpsimd, "pool"),
            "sp": Eng(self, nc.sync, "sp"),
        }
        self.dma_sems = [[self.new_sem(f"dma{i}"), 0] for i in range(n_dma_sems)]
        self.dma_i = 0
        self.cc_sems = [[self.new_sem(f"cc{i}"), 0] for i in range(56)]
        self.cc_i = 0
        self.psb = []
        self.ps_i = 0

    def new_sem(self, name):
        self._semn += 1
        return self.root_es.enter_context(self.nc.semaphore(f"{name}_{self._semn}"))

    def sb(self, name, shape, dt):
        return self.es.enter_context(self.nc.sbuf_tensor(uname(name), shape, dt))

    def ps(self, name, shape, dt=F32):
        return self.es.enter_context(self.nc.psum_tensor(uname(name), shape, dt))

    def init_psum(self, nf32=8, nbf=0, nacc=0):
        self.psb = [T(self.ps(f"ps{i}", [128, 512], F32)[:], f"ps{i}") for i in range(nf32)]
        self.psa = [T(self.ps(f"psa{i}", [128, 512], F32)[:], f"psa{i}") for i in range(nacc)]
        self.psa_i = 0
        self.psb16 = [T(self.ps(f"psh{i}", [128, 1024], BF16)[:], f"psh{i}") for i in range(nbf)]
        self.ps16_i = 0

    def next_ps(self):
        t = self.psb[self.ps_i % len(self.psb)]
        self.ps_i += 1
        return t

    def next_acc(self):
        t = self.psa[self.psa_i % len(self.psa)]
        self.psa_i += 1
        return t

    def next_ps16(self):
        t = self.psb16[self.ps16_i % len(self.psb16)]
        self.ps16_i += 1
        return t

    def tr(self, ps, out_ap, in_t, in_ap, ident_t):
        nc = self.nc
        return self.op("pe", lambda: nc.tensor.transpose(out_ap, in_ap, ident_t.ap), reads=[in_t, ident_t], writes=[ps])

    def cp(self, en, out_t, in_t, out_ap=None, in_ap=None):
        o = out_t.ap if out_ap is None else out_ap
        i = in_t.ap if in_ap is None else in_ap
        if en == "act":
            return self.op("act", lambda: self.nc.scalar.copy(out=o, in_=i), reads=[in_t], writes=[out_t])
        eng = self.nc.vector if en == "dve" else self.nc.gpsimd
        return self.op(en, lambda: eng.tensor_copy(out=o, in_=i), reads=[in_t], writes=[out_t])

    def _deps(self, reads, writes):
        deps = []
        for t in reads:
            if t.lw is not None:
                deps.append(t.lw)
        for t in writes:
            for e in self.E.values():
                if e.pend_r and any(t is p for p in e.pend_r):
                    raise RuntimeError(f"write to {t.name} while un-flushed reads pending on {e.name}")
            if t.lw is not None:
                deps.append(t.lw)
            deps.extend(t.rd.items())
        return deps

    def op(self, en, fn, reads=(), writes=(), inc=True, chk_w=True):
        e = self.E[en]
        deps = self._deps(reads, writes if chk_w else ())
        for sem, val in deps:
            if en == "pe" and sem is e.sem:
                continue
            e.wait(sem, val)
        inst = fn()
        if inc:
            tok = e.bump(inst)
            for t in e.pend_r:
                t.rd[tok[0]] = tok[1]
            for t in reads:
                t.rd[tok[0]] = tok[1]
            for t in e.pend_w:
                t.lw = tok
                t.rd = {}
            for t in writes:
                t.lw = tok
                t.rd = {}
            e.pend_r = []
            e.pend_w = []
        else:
            e.pend_r.extend(reads)
            e.pend_w.extend(writes)
        return inst

    def dma(self, en, out_ap, in_ap, reads=(), writes=(), **kw):
        e = self.E[en]
        for sem, val in self._deps(reads, writes):
            e.wait(sem, val)
        slot = self.dma_sems[self.dma_i % len(self.dma_sems)]
        self.dma_i += 1
        if slot[1] > 0:
            e.wait(slot[0], slot[1])
        inst = e.eng.dma_start(out=out_ap, in_=in_ap, **kw)
        slot[1] += 16
        inst.then_inc(slot[0], 16)
        tok = (slot[0], slot[1])
        for t in writes:
            t.lw = tok
            t.rd = {}
        for t in reads:
            t.rd[tok[0]] = tok[1]
        return tok

    def allgather(self, in_ap, out_ap, groups):
        e = self.E["pool"]
        slot = self.cc_sems[self.cc_i % len(self.cc_sems)]
        self.cc_i += 1
        if slot[1] > 0:
            e.wait(slot[0], slot[1])
        inst = self.nc.gpsimd.collective_compute("AllGather", ALU.bypass, replica_groups=groups, ins=[in_ap], outs=[out_ap])
        slot[1] += 1
        inst.then_inc(slot[0], 1)
        return (slot[0], slot[1])

    def wait_tokens(self, toks):
        for e in self.E.values():
            for sem, val in toks:
                e.wait(sem, val)

    def wait_cc(self):
        for e in self.E.values():
            for sem, val in self.cc_sems:
                if val > 0:
                    e.wait(sem, val)

    def finish(self, en="sp"):
        e = self.E[en]
        for sem, val in self.dma_sems:
            if val > 0:
                e.wait(sem, val)
        for n, o in self.E.items():
            if o is not e and o.count > 0:
                e.wait(o.sem, o.count)

    def mm(self, ps, lhsT_ap, rhs_ap, reads, start, stop, out_ap=None):
        nc = self.nc
        oap = ps.ap if out_ap is None else out_ap
        return self.op("pe", lambda: nc.tensor.matmul(oap, lhsT=lhsT_ap, rhs=rhs_ap, start=start, stop=stop),
                       reads=reads, writes=[ps], inc=stop, chk_w=start)

    def act(self, out_t, in_t, func, out_ap=None, in_ap=None, extra_reads=(), **kw):
        nc = self.nc
        o = out_t.ap if out_ap is None else out_ap
        i = in_t.ap if in_ap is None else in_ap
        return self.op("act", lambda: nc.scalar.activation(out=o, in_=i, func=func, **kw),
                       reads=[in_t] + list(extra_reads), writes=[out_t])

    def tt(self, en, out_t, a_t, b_t, op, out_ap=None, a_ap=None, b_ap=None):
        eng = self.nc.vector if en == "dve" else self.nc.gpsimd
        o = out_t.ap if out_ap is None else out_ap
        a = a_t.ap if a_ap is None else a_ap
        b = b_t.ap if b_ap is None else b_ap
        return self.op(en, lambda: eng.tensor_tensor(out=o, in0=a, in1=b, op=op), reads=[a_t, b_t], writes=[out_t])

    def ts(self, en, out_t, a_t, s1, s2, op0, op1=None, out_ap=None, a_ap=None, extra_reads=()):
        eng = self.nc.vector if en == "dve" else self.nc.gpsimd
        o = out_t.ap if out_ap is None else out_ap
        a = a_t.ap if a_ap is None else a_ap
        if op1 is None:
            f = lambda: eng.tensor_scalar(out=o, in0=a, scalar1=s1, scalar2=None, op0=op0)
        else:
            f = lambda: eng.tensor_scalar(out=o, in0=a, scalar1=s1, scalar2=s2, op0=op0, op1=op1)
        return self.op(en, f, reads=[a_t] + list(extra_reads), writes=[out_t])

    def stt(self, en, out_t, a_t, scalar, b_t, op0, op1, out_ap=None, a_ap=None, b_ap=None, extra_reads=()):
        eng = self.nc.vector if en == "dve" else self.nc.gpsimd
        o = out_t.ap if out_ap is None else out_ap
        a = a_t.ap if a_ap is None else a_ap
        b = b_t.ap if b_ap is None else b_ap
        return self.op(en, lambda: eng.scalar_tensor_tensor(out=o, in0=a, scalar=scalar, in1=b, op0=op0, op1=op1),
                       reads=[a_t, b_t] + list(extra_reads), writes=[out_t])


class Dense:
    def __init__(self, kb):
        self.kb = kb
        nc = kb.nc
        self.hbuf = kb.sb("hbuf", [128, KC, NT], F32)
        self.h = [T(self.hbuf[:, k, :], f"h{k}") for k in range(KC)]
        xnb = kb.sb("xn", [128, KC, NT], BF16)
        self.xn = [T(xnb[:, k, :], f"xn{k}") for k in range(KC)]
        hidb = kb.sb("hid", [128, FC, NT], BF16)
        self.hid = [T(hidb[:, k, :], f"hid{k}") for k in range(FC)]
        self.wsl = [T(kb.sb(f"w{i}", [128, KC, 512], BF16)[:], f"w{i}") for i in range(3)]
        self.wi = 0
        self.stg = [T(kb.sb(f"stg{i}", [128, NT], F32)[:], f"stg{i}") for i in range(4)]
        self.si = 0
        self.stgb = [T(kb.sb(f"stgb{i}", [128, NT], BF16)[:], f"stgb{i}") for i in range(4)]
        self.sbi = 0
        self.tmp = [T(kb.sb(f"tmp{i}", [128, NT], F32)[:], f"tmp{i}") for i in range(4)]
        self.ti = 0
        self.sqb = [T(kb.sb(f"sqb{i}", [128, NT], BF16)[:], f"sqb{i}") for i in range(2)]
        self.qi = 0
        self.rstd = T(kb.sb("rstd", [128, NT], F32)[:], "rstd")
        onesb = kb.sb("ones", [128, 128], BF16)
        self.ones = T(onesb[:], "ones")
        kb.op("dve", lambda: nc.vector.memset(onesb[:], 1.0), writes=[self.ones])

    def wslot(self):
        t = self.wsl[self.wi % len(self.wsl)]
        self.wi += 1
        return t

    def stage(self, bf=False):
        if bf:
            t = self.stgb[self.sbi % len(self.stgb)]
            self.sbi += 1
            return t
        t = self.stg[self.si % len(self.stg)]
        self.si += 1
        return t

    def temp(self):
        t = self.tmp[self.ti % len(self.tmp)]
        self.ti += 1
        return t

    def sq(self):
        t = self.sqb[self.qi % len(self.sqb)]
        self.qi += 1
        return t

    def const(self, name, dram_ap, shape, dt=F32):
        t = T(self.kb.sb(name, shape, dt)[:], name)
        self.kb.dma("sp", t.ap, dram_ap, writes=[t])
        return t

    def rms_rstd(self, chunks, dim, out_rstd, in_aps=None):
        kb = self.kb
        ps = kb.next_ps()
        n = len(chunks)
        for k in range(n):
            sq = self.sq()
            kb.act(sq, chunks[k], AF.Square, in_ap=None if in_aps is None else in_aps[k])
            kb.op("pe", lambda: kb.nc.tensor.matmul(ps.ap, lhsT=self.ones.ap, rhs=sq.ap, start=(k == 0), stop=(k == n - 1)),
                  reads=[self.ones, sq], writes=[ps], inc=True, chk_w=(k == 0))
        kb.ts("dve", out_rstd, ps, 1.0 / dim, EPS, ALU.mult, ALU.add)
        kb.act(out_rstd, out_rstd, AF.Sqrt)
        kb.op("dve", lambda: kb.nc.vector.reciprocal(out=out_rstd.ap, in_=out_rstd.ap), reads=[out_rstd], writes=[out_rstd])

    def rmsnorm_xn(self, gain):
        kb = self.kb
        self.rms_rstd(self.h, D, self.rstd)
        for k in range(KC):
            kb.stt("dve", self.xn[k], self.h[k], gain.ap[:, k:k + 1], self.rstd, ALU.mult, ALU.mult, extra_reads=[gain])

    def linear(self, srcs, nchunks, evac):
        kb = self.kb
        views = [(xc, w.rearrange("(kc p) n -> p kc n", p=128)) for xc, w in srcs]
        for nb in range((nchunks + 3) // 4):
            ncb = min(4, nchunks - nb * 4)
            slots = []
            for xc, wv in views:
                s = self.wslot()
                kk = len(xc)
                kb.dma("pool", s.ap[:, 0:kk, 0:ncb * 128], wv[:, :, nb * 512:nb * 512 + ncb * 128], writes=[s])
                slots.append(s)
            for c in range(ncb):
                pss = []
                for (xc, wv), s in zip(views, slots):
                    ps = kb.next_ps()
                    kk = len(xc)
                    for k in range(kk):
                        kb.mm(ps, s.ap[:, k, c * 128:(c + 1) * 128], xc[k].ap, [s, xc[k]], k == 0, k == kk - 1)
                    pss.append(ps)
                evac(nb * 4 + c, pss)

    def ffn(self, gain, wg, wu, wd):
        kb = self.kb
        nc = kb.nc
        self.rmsnorm_xn(gain)
        hid = self.hid

        def ev(j, pss):
            kb.act(hid[j], pss[0], AF.Silu)
            kb.tt("dve", hid[j], hid[j], pss[1], ALU.mult)
        self.linear([(self.xn, wg), (self.xn, wu)], FC, ev)
        wdv = wd.rearrange("(kc p) n -> p kc n", p=128)
        for og in range(4):
            pos = [kb.next_ps() for _ in range(4)]
            for kg in range(4):
                sd = self.wslot()
                kb.dma("pool", sd.ap[:, 0:11, :], wdv[:, kg * 11:(kg + 1) * 11, og * 512:(og + 1) * 512], writes=[sd])
                for c in range(4):
                    for kk in range(11):
                        k = kg * 11 + kk
                        kb.op("pe", lambda: nc.tensor.matmul(pos[c].ap, lhsT=sd.ap[:, kk, c * 128:(c + 1) * 128], rhs=hid[k].ap,
                                                             start=(k == 0), stop=(k == FC - 1)),
                              reads=[sd, hid[k]], writes=[pos[c]], inc=(k == FC - 1 or kk == 10), chk_w=(k == 0))
            for c in range(4):
                o = og * 4 + c
                kb.stt("dve", self.h[o], pos[c], 0.5, self.h[o], ALU.mult, ALU.add)


def phase_A(dn, t, P, outs):
    kb = dn.kb
    nc = kb.nc
    cols = slice(t * NT, (t + 1) * NT)
    dn.ffn(P["ffn1_g"], P["ffn1_wg"], P["ffn1_wu"], P["ffn1_wd"])
    h1v = outs["h1T"].rearrange("(kc p) n -> p kc n", p=128)
    kb.dma("sp", h1v[:, :, cols], dn.hbuf[:], reads=dn.h)
    store_ap = outs.get("store_ap")
    is_bf = outs.get("is_bf", lambda j: False)
    dn.rmsnorm_xn(P["mix_g"])
    projT = outs.get("projT")
    w_in = P["w_in"]
    cos_t = P["cos_t"]; sin_t = P["sin_t"]
    kb.dma("sp", cos_t.ap, P["cosT"][:, cols], writes=[cos_t])
    kb.dma("sp", sin_t.ap, P["sinT"][:, cols], writes=[sin_t])

    def store(j, st):
        dst = projT[j * 128:(j + 1) * 128, cols] if store_ap is None else store_ap(j, cols)
        kb.dma("sp", dst, st.ap, reads=[st])

    def seg_plain(c0, c1):
        def ev(j, pss):
            st = dn.stage(is_bf(c0 + j))
            if j % 2 == 0:
                kb.act(st, pss[0], AF.Copy)
            else:
                kb.op("dve", lambda: nc.vector.tensor_copy(out=st.ap, in_=pss[0].ap), reads=[pss[0]], writes=[st])
            store(c0 + j, st)
        dn.linear([(dn.xn, w_in[:, c0 * 128:c1 * 128])], c1 - c0, ev)

    def seg_qk(c0, c1, gcol):
        def ev(j, pss):
            rs = dn.temp()
            dn.rms_rstd([pss[0]], 128, rs)
            st = dn.stage(is_bf(c0 + j))
            kb.stt("dve", st, pss[0], gcol.ap[:, 0:1], rs, ALU.mult, ALU.mult, extra_reads=[gcol])
            store(c0 + j, st)
        dn.linear([(dn.xn, w_in[:, c0 * 128:c1 * 128])], c1 - c0, ev)

    def seg_rot(c0, c1, sw0, scale):
        def ev(j, pss):
            t1 = dn.temp()
            kb.stt("dve", t1, pss[0], scale, cos_t, ALU.mult, ALU.mult)
            t2 = dn.temp()
            kb.stt("dve", t2, pss[1], scale, sin_t, ALU.mult, ALU.mult)
            st = dn.stage(is_bf(c0 + j))
            kb.tt("dve", st, t2, t1, ALU.add)
            store(c0 + j, st)
        dn.linear([(dn.xn, w_in[:, c0 * 128:c1 * 128]), (dn.xn, P["w_in_sw"][:, sw0 * 128:(sw0 + c1 - c0) * 128])], c1 - c0, ev)

    seg_plain(*CH_XP)
    seg_qk(CH_AQ[0], CH_AQ[1], P["qg"])
    seg_qk(CH_AK[0], CH_AK[1], P["kg"])
    seg_plain(*CH_AV)
    seg_rot(CH_RQ[0], CH_RQ[1], 0, 1.0)
    seg_rot(CH_RK[0], CH_RK[1], 4, 128.0 ** -0.5)
    seg_plain(CH_RV[0], CH_CIN[1])


def phase_C(dn, t, P):
    kb = dn.kb
    nc = kb.nc
    cols = slice(t * NT, (t + 1) * NT)
    dn.rmsnorm_xn(P["mix_g"])
    merged = P["mrg"]
    macc = P["macc"]
    brs = [("ypool", "w_br_pool", 0, 8), ("yattn", "w_br_attn", 8, 12), ("yret", "w_br_ret", 20, 8), ("yconv", "w_br_conv", 28, 8)]
    yload = P.get("yload")
    for (yn, wn, h0, kk) in brs:
        for k in range(kk):
            src = P[yn][k * 128:(k + 1) * 128, cols] if yload is None else yload(yn, k, t)
            kb.dma("sp", dn.hid[h0 + k].ap, src, writes=[dn.hid[h0 + k]])
    for nb in range(4):
        for i, (yn, wn, h0, kk) in enumerate(brs):
            ych = dn.hid[h0:h0 + kk]

            def ev(c, pss, i=i, nb=nb):
                o = nb * 4 + c
                g = dn.temp()
                kb.act(g, pss[0], AF.Sigmoid, bias=P["b_gate"].ap[:, i * 16 + o:i * 16 + o + 1], extra_reads=[P["b_gate"]])
                if i == 0:
                    kb.tt("dve", macc[c], g, pss[1], ALU.mult)
                else:
                    kb.tt("dve", g, g, pss[1], ALU.mult)
                    if i < 3:
                        kb.tt("dve", macc[c], macc[c], g, ALU.add)
                    else:
                        kb.tt("dve", merged[o], macc[c], g, ALU.add)
            dn.linear([(dn.xn, P["w_gate"][:, i * D + nb * 512:i * D + (nb + 1) * 512]),
                       (ych, P[wn][:, nb * 512:(nb + 1) * 512])], 4, ev)

    def ev_out(o, pss):
        kb.tt("dve", dn.h[o], dn.h[o], pss[0], ALU.add)
    dn.linear([(merged, P["w_out"])], KC, ev_out)
    dn.ffn(P["ffn2_g"], P["ffn2_wg"], P["ffn2_wu"], P["ffn2_wd"])
    dn.rmsnorm_xn(P["ple_g"])
    p32 = P["p32"]; pbf = dn.hid[0:2]
    pv = P["pT"].rearrange("(kc p) n -> p kc n", p=128)
    for k in range(2):
        kb.dma("sp", p32[k].ap, pv[:, k, cols], writes=[p32[k]])
        kb.act(pbf[k], p32[k], AF.Copy)

    def ev_ple(o, pss):
        g = dn.temp()
        kb.act(g, pss[0], AF.Sigmoid)
        kb.tt("dve", g, g, pss[1], ALU.mult)
        kb.tt("dve", dn.h[o], dn.h[o], g, ALU.add)
    dn.linear([(dn.xn, P["w_ple_gate"]), (pbf, P["w_ple_proj"])], KC, ev_ple)


A_PARAMS = [("ffn1_g", [128, KC]), ("mix_g", [128, KC]), ("qn", [128, 1]), ("kn", [128, 1]),
            ("ffn1_wg", [D, DFF]), ("ffn1_wu", [D, DFF]), ("ffn1_wd", [DFF, D]), ("w_in", [D, IN_W]), ("w_in_sw", [D, 1024]),
            ("cosT", [128, TPC]), ("sinT", [128, TPC])]
C_PARAMS = [("mix_g", [128, KC]), ("ffn2_g", [128, KC]), ("ple_g", [128, KC]), ("b_gate", [128, 64]),
            ("w_gate", [D, 4 * D]), ("w_br_pool", [1024, D]), ("w_br_attn", [1536, D]), ("w_br_ret", [1024, D]),
            ("w_br_conv", [1024, D]), ("w_out", [D, D]), ("ffn2_wg", [D, DFF]), ("ffn2_wu", [D, DFF]), ("ffn2_wd", [DFF, D]),
            ("w_ple_gate", [D, D]), ("w_ple_proj", [256, D]), ("pT", [256, TPC])]
C_ACTS = [("ypool", [1024, TPC]), ("yattn", [1536, TPC]), ("yret", [1024, TPC]), ("yconv", [1024, TPC])]
SMALL = {"ffn1_g", "mix_g", "qn", "kn", "ffn2_g", "ple_g", "b_gate"}


def build_dense(doC, doA, ntiles=TPC // NT):
    nc = bass.Bass("TRN2", target_bir_lowering=False)
    hin = nc.dram_tensor("hin", [D, TPC], F32, kind="ExternalInput").ap()
    PA = {}; PC = {}
    if doC:
        for n, shp in C_PARAMS:
            PC[n] = nc.dram_tensor("c_" + n, shp, F32, kind="ExternalInput").ap()
        for n, shp in C_ACTS:
            PC[n] = nc.dram_tensor("c_" + n, shp, BF16, kind="ExternalInput").ap()
    if doA:
        for n, shp in A_PARAMS:
            PA[n] = nc.dram_tensor("a_" + n, shp, F32, kind="ExternalInput").ap()
    outs = {}
    if doA:
        outs["h1T"] = nc.dram_tensor("h1T", [D, TPC], F32, kind="ExternalOutput").ap()
        outs["projT"] = nc.dram_tensor("projT", [IN_W, TPC], F32, kind="ExternalOutput").ap()
    else:
        outs["hout"] = nc.dram_tensor("hout", [D, TPC], F32, kind="ExternalOutput").ap()
    with ExitStack() as es:
        es.enter_context(nc.allow_low_precision("bf16 matmul operands, fp32 accumulation"))
        kb = KB(nc, es)
        kb.init_psum()
        dn = Dense(kb)
        if doC:
            for n in ("mix_g", "ffn2_g", "ple_g", "b_gate"):
                PC[n] = dn.const("c_sb_" + n, PC[n], list(PC[n].shape))
            PC["macc"] = [T(kb.sb(f"macc{o}", [128, NT], F32)[:], f"macc{o}") for o in range(4)]
            mrgb = kb.sb("mrg", [128, KC, NT], BF16)
            PC["mrg"] = [T(mrgb[:, k, :], f"mrg{k}") for k in range(KC)]
            PC["p32"] = [T(kb.sb(f"p32_{k}", [128, NT], F32)[:], f"p32_{k}") for k in range(2)]
        if doA:
            for n in ("ffn1_g", "mix_g", "qn", "kn"):
                PA[n] = dn.const("a_sb_" + n, PA[n], list(PA[n].shape))
            qg = T(kb.sb("qg", [128, 1], F32)[:], "qg")
            kb.ts("dve", qg, PA["qn"], 128.0 ** -0.5, None, ALU.mult)
            PA["qg"] = qg
            PA["cos_t"] = T(kb.sb("cos_t", [128, NT], F32)[:], "cos_t")
            PA["sin_t"] = T(kb.sb("sin_t", [128, NT], F32)[:], "sin_t")
            PA["kg"] = PA["kn"]
        hv = hin.rearrange("(kc p) n -> p kc n", p=128)
        for t in range(ntiles):
            cols = slice(t * NT, (t + 1) * NT)
            kb.dma("sp", dn.hbuf[:], hv[:, :, cols], writes=dn.h)
            if doC:
                phase_C(dn, t, PC)
            if doA:
                phase_A(dn, t, PA, outs)
            else:
                ov = outs["hout"].rearrange("(kc p) n -> p kc n", p=128)
                kb.dma("sp", ov[:, :, cols], dn.hbuf[:], reads=dn.h)
        kb.finish("sp")
    return nc


POOL_WIN = (2, 4, 8, 16)
PH = 8
CH = 15
BIG_NEG = -30000.0


def kb_barrier(kb):
    for e in kb.E.values():
        for sem, val in kb.dma_sems:
            if val > 0:
                e.wait(sem, val)
        for o in kb.E.values():
            if o is not e and o.count > 0:
                e.wait(o.sem, o.count)


def mixer_pool(kb, xp_h, invcnt, pool_w, pool_scale, ypool, xload=None):
    nc = kb.nc
    TT_ = 2048
    W = TT_ + 2 * PH
    with ExitStack() as sub:
        sb = lambda n, s, d: sub.enter_context(nc.sbuf_tensor(uname(n), s, d))
        xb = [T(sb(f"pl_x{i}", [128, W], F32)[:], f"pl_x{i}") for i in range(2)]
        ab = [T(sb(f"pl_a{i}", [128, W], F32)[:], f"pl_a{i}") for i in range(2)]
        ic = T(sb("pl_ic", [128, TT_], F32)[:], "pl_ic")
        mx = [T(sb(f"pl_m{i}", [128, TT_], BF16)[:], f"pl_m{i}") for i in range(2)]
        wsb = T(sb("pl_w", [128, 8, 256], BF16)[:], "pl_w")
        psc = T(sb("pl_sc", [128, 8], F32)[:], "pl_sc")
        stg = [T(sb(f"pl_o{i}", [128, 512], BF16)[:], f"pl_o{i}") for i in range(3)]
        kb.dma("pool", wsb.ap, pool_w.rearrange("(k p) n -> p k n", p=128), writes=[wsb])
        kb.dma("sp", psc.ap, pool_scale, writes=[psc])
        si = 0
        for tt in range(TPC // TT_):
            for g in range(4):
                w = POOL_WIN[g]
                kb.dma("sp", ic.ap, invcnt[g:g + 1, tt * TT_:(tt + 1) * TT_].partition_broadcast(128), writes=[ic])
                for kc in range(2):
                    ch = g * 2 + kc
                    x = xb[kc]
                    if xload is None:
                        kb.dma("sp", x.ap, xp_h[ch * 128:(ch + 1) * 128, tt * TT_:tt * TT_ + W], writes=[x])
                    else:
                        xload(x, ch * 128, tt * TT_ - PH, tt * TT_ + TT_ + PH)
                    cur = x
                    sh = 1
                    pp = 0
                    while sh < w:
                        nxt = ab[pp]; pp ^= 1
                        lo = 2 * sh - 1
                        kb.tt("dve", nxt, cur, cur, ALU.add, out_ap=nxt.ap[:, lo:W], a_ap=cur.ap[:, lo:W], b_ap=cur.ap[:, lo - sh:W - sh])
                        cur = nxt
                        sh *= 2
                    off = PH + w // 2 - 1
                    tmp = ab[pp]
                    kb.tt("dve", tmp, cur, ic, ALU.mult, out_ap=tmp.ap[:, 0:TT_], a_ap=cur.ap[:, off:off + TT_])
                    kb.tt("dve", mx[kc], tmp, x, ALU.subtract, a_ap=tmp.ap[:, 0:TT_], b_ap=x.ap[:, PH:PH + TT_])
                for q in range(TT_ // 512):
                    for dc in range(2):
                        ps = kb.next_ps()
                        for kc in range(2):
                            kb.mm(ps, wsb.ap[:, g * 2 + kc, dc * 128:(dc + 1) * 128], mx[kc].ap[:, q * 512:(q + 1) * 512],
                                  [wsb, mx[kc]], kc == 0, kc == 1)
                        st = stg[si % 3]; si += 1
                        ch = g * 2 + dc
                        kb.ts("dve", st, ps, psc.ap[:, ch:ch + 1], None, ALU.mult, extra_reads=[psc])
                        c0 = tt * TT_ + q * 512
                        kb.dma("sp", ypool[ch * 128:(ch + 1) * 128, c0:c0 + 512], st.ap, reads=[st])
        kb_barrier(kb)


def mixer_conv(kb, cin_h, conv_w, conv_b, norm_g, norm_b, ident, yconv, xload=None):
    nc = kb.nc
    W = 512 + 2 * CH
    with ExitStack() as sub:
        sb = lambda n, s, d: sub.enter_context(nc.sbuf_tensor(uname(n), s, d))
        cw = T(sb("cv_w", [128, 8, 31], F32)[:], "cv_w")
        cb = T(sb("cv_b", [128, 8], F32)[:], "cv_b")
        ng = T(sb("cv_g", [128, 8], F32)[:], "cv_g")
        nb = T(sb("cv_nb", [128, 8], F32)[:], "cv_nb")
        idt = T(sb("cv_id", [128, 128], F32)[:], "cv_id")
        diag = T(sb("cv_diag", [128, 8 * 31, 128], BF16)[:], "cv_diag")
        ones = T(sb("cv_ones", [128, 128], BF16)[:], "cv_ones")
        a32 = [T(sb(f"cv_a{i}", [128, W], F32)[:], f"cv_a{i}") for i in range(2)]
        g32 = [T(sb(f"cv_g32{i}", [128, W], F32)[:], f"cv_g32{i}") for i in range(2)]
        hg = [T(sb(f"cv_h{i}", [128, W], BF16)[:], f"cv_h{i}") for i in range(2)]
        xc = [T(sb(f"cv_x{i}", [128, 512], F32)[:], f"cv_x{i}") for i in range(8)]
        xbf = [T(sb(f"cv_xb{i}", [128, 512], BF16)[:], f"cv_xb{i}") for i in range(2)]
        mean = T(sb("cv_mean", [128, 512], F32)[:], "cv_mean")
        rstd = T(sb("cv_rstd", [128, 512], F32)[:], "cv_rstd")
        msq = T(sb("cv_msq", [128, 512], F32)[:], "cv_msq")
        stg = [T(sb(f"cv_o{i}", [128, 512], BF16)[:], f"cv_o{i}") for i in range(3)]
        for t_, src in ((cw, conv_w), (cb, conv_b), (ng, norm_g), (nb, norm_b), (idt, ident)):
            kb.dma("sp", t_.ap, src, writes=[t_])
        kb.op("dve", lambda: nc.vector.memset(ones.ap, 1.0), writes=[ones])
        for ch in range(8):
            for j in range(31):
                kb.ts("dve" if (ch * 31 + j) % 2 == 0 else "pool", diag, idt, cw.ap[:, ch, j:j + 1], None, ALU.mult,
                      out_ap=diag.ap[:, ch * 31 + j, :], extra_reads=[cw])
        si = 0
        for tt in range(TPC // 512):
            c0 = tt * 512
            for ch in range(8):
                a = a32[ch % 2]; g = g32[ch % 2]; h = hg[ch % 2]
                if xload is None:
                    kb.dma("sp", a.ap, cin_h[ch * 128:(ch + 1) * 128, c0:c0 + W], writes=[a])
                    kb.dma("sp", g.ap, cin_h[1024 + ch * 128:1024 + (ch + 1) * 128, c0:c0 + W], writes=[g])
                else:
                    xload(a, 1024 + ch * 128, c0 - CH, c0 + 512 + CH)
                    xload(g, 2048 + ch * 128, c0 - CH, c0 + 512 + CH)
                kb.act(g, g, AF.Sigmoid)
                kb.tt("dve", h, a, g, ALU.mult)
                ps = kb.next_ps()
                for j in range(31):
                    kb.mm(ps, diag.ap[:, ch * 31 + j, :], h.ap[:, j:j + 512], [diag, h], j == 0, j == 30)
                kb.ts("dve", xc[ch], ps, cb.ap[:, ch:ch + 1], None, ALU.add, extra_reads=[cb])
            ps_m = kb.next_ps()
            ps_q = kb.next_ps()
            for ch in range(8):
                xb_ = xbf[ch % 2]
                kb.act(xb_, xc[ch], AF.Copy)
                kb.op("pe", lambda: nc.tensor.matmul(ps_m.ap, lhsT=ones.ap, rhs=xb_.ap, start=(ch == 0), stop=(ch == 7)),
                      reads=[ones, xb_], writes=[ps_m], inc=True, chk_w=(ch == 0))
            for ch in range(8):
                xb_ = xbf[ch % 2]
                kb.act(xb_, xc[ch], AF.Square)
                kb.op("pe", lambda: nc.tensor.matmul(ps_q.ap, lhsT=ones.ap, rhs=xb_.ap, start=(ch == 0), stop=(ch == 7)),
                      reads=[ones, xb_], writes=[ps_q], inc=True, chk_w=(ch == 0))
            kb.ts("dve", mean, ps_m, 1.0 / 1024, None, ALU.mult)
            kb.tt("dve", msq, mean, mean, ALU.mult)
            kb.stt("dve", rstd, ps_q, 1.0 / 1024, msq, ALU.mult, ALU.subtract)
            kb.ts("dve", rstd, rstd, EPS, None, ALU.add)
            kb.act(rstd, rstd, AF.Sqrt)
            kb.op("dve", lambda: nc.vector.reciprocal(out=rstd.ap, in_=rstd.ap), reads=[rstd], writes=[rstd])
            for ch in range(8):
                kb.tt("dve", xc[ch], xc[ch], mean, ALU.subtract)
                kb.tt("dve", xc[ch], xc[ch], rstd, ALU.mult)
                st = stg[si % 3]; si += 1
                kb.act(st, xc[ch], AF.Silu, scale=ng.ap[:, ch:ch + 1], bias=nb.ap[:, ch:ch + 1], extra_reads=[ng, nb])
                kb.dma("sp", yconv[ch * 128:(ch + 1) * 128, c0:c0 + 512], st.ap, reads=[st])
        kb_barrier(kb)


def mixer_ret(kb, qT, kT, vT, gT, dlog, rnorm, C, crossb, yret, xdt=F32, mid_hook=None):
    nc = kb.nc
    NB = SEQ // 512
    with ExitStack() as sub:
        sb = lambda n, s, d: sub.enter_context(nc.sbuf_tensor(uname(n), s, d))
        mk = lambda n, s, d=F32: T(sb("rt_" + n, s, d)[:], "rt_" + n)
        idt32 = mk("id32", [128, 128]); idt = mk("id", [128, 128], BF16)
        dpos = mk("dpos", [128, 128]); dneg = mk("dneg", [128, 128]); mF = mk("mF", [128, 128]); mB = mk("mB", [128, 128])
        irow = mk("irow", [128, 128]); icol = mk("icol", [128, 1])
        for t_, n in ((idt32, "ident"), (dpos, "dpos"), (dneg, "dneg"), (mF, "maskF"), (mB, "maskB"), (irow, "idx_row"), (icol, "idx_col")):
            kb.dma("sp", t_.ap, C[n], writes=[t_])
        kb.cp("dve", idt, idt32)
        ones = mk("ones", [128, 128], BF16)
        kb.op("dve", lambda: nc.vector.memset(ones.ap, 1.0), writes=[ones])
        rn = mk("rn", [128, 2])
        kb.dma("sp", rn.ap, rnorm, writes=[rn])
        LG = mk("LG", [128, 2]); NL = mk("NL", [128, 2]); LG128 = mk("LG128", [128, 2]); LG127 = mk("LG127", [128, 2]); gch = mk("gch", [128, 2])
        kb.dma("sp", LG.ap, dlog.partition_broadcast(128), writes=[LG])
        kb.act(NL, LG, AF.Exp, scale=-1.0)
        kb.act(NL, NL, AF.Ln, bias=1.0)
        kb.ts("dve", LG, NL, -1.0, None, ALU.mult)
        kb.ts("dve", LG128, LG, 128.0, None, ALU.mult)
        kb.ts("dve", LG127, LG, 127.0, None, ALU.mult)
        kb.act(gch, LG128, AF.Exp)
        DcT = mk("DcT", [128, 128]); etmp = mk("etmp", [128, 128])
        kb.act(DcT, dpos, AF.Exp, scale=LG.ap[:, 0:1], extra_reads=[LG])
        kb.tt("dve", DcT, DcT, mF, ALU.mult)
        kb.act(etmp, dneg, AF.Exp, scale=LG.ap[:, 1:2], extra_reads=[LG])
        kb.tt("dve", etmp, etmp, mB, ALU.mult)
        kb.tt("dve", DcT, DcT, etmp, ALU.add)
        XiF = mk("XiF", [128, 512]); XiB = mk("XiB", [128, 512])
        for c in range(4):
            kb.act(XiF, irow, AF.Exp, out_ap=XiF.ap[:, c * 128:(c + 1) * 128], scale=LG.ap[:, 0:1], bias=LG.ap[:, 0:1], extra_reads=[LG])
            kb.act(XiB, irow, AF.Exp, out_ap=XiB.ap[:, c * 128:(c + 1) * 128], scale=NL.ap[:, 1:2], bias=LG128.ap[:, 1:2], extra_reads=[NL, LG128])
        zF = mk("zF", [128, 1]); zB = mk("zB", [128, 1])
        kb.act(zF, icol, AF.Exp, scale=NL.ap[:, 0:1], bias=LG127.ap[:, 0:1], extra_reads=[NL, LG127])
        kb.act(zB, icol, AF.Exp, scale=LG.ap[:, 1:2], extra_reads=[LG])
        S32 = mk("S32", [128, 256]); Sbfs = [mk("SbfA", [128, 256], BF16), mk("SbfB", [128, 256], BF16)]
        sbi = [0]
        q32 = [mk(f"q32{i}", [128, 512], xdt) for i in range(2)]; k32 = [mk(f"k32{i}", [128, 512], xdt) for i in range(2)]
        v32 = [mk(f"v32{i}", [128, 2, 512], xdt) for i in range(2)]; g32 = [mk(f"g32{i}", [128, 2, 512]) for i in range(2)]
        cb32 = [mk(f"cb32{i}", [128, 2, 512]) for i in range(2)]
        q16 = mk("q16", [128, 512], BF16); qx = mk("qx", [128, 512], BF16); k16 = mk("k16", [128, 512], BF16)
        v16 = mk("v16", [128, 2, 512], BF16)
        kz = [mk(f"kz{i}", [128, 128], BF16) for i in range(2)]; vtm = [mk(f"vtm{i}", [128, 256], BF16) for i in range(2)]
        Pm = [mk(f"P{i}", [128, 128], BF16) for i in range(2)]
        yv = [mk(f"y{i}", [128, 512]) for i in range(2)]; sqb = [mk(f"sq{i}", [128, 512], BF16) for i in range(2)]
        rstd = mk("rstd", [128, 512]); sg = mk("sg", [128, 512])
        ost = [mk(f"ost{i}", [128, 512], BF16) for i in range(2)]; cst = [mk(f"cst{i}", [128, 512]) for i in range(2)]
        if callable(qT):
            rsrc = qT
        else:
            vv = vT.rearrange("(v p) n -> p v n", p=128); gv = gT.rearrange("(v p) n -> p v n", p=128)
            rsrc = lambda nm, cols: {"q": lambda: qT[:, cols], "k": lambda: kT[:, cols], "v": lambda: vv[:, :, cols], "g": lambda: gv[:, :, cols]}[nm]()
        cbv = crossb.rearrange("(v p) n -> p v n", p=128)
        cbT = [T(None, f"cb{bi}") for bi in range(NB)]
        ci = 0

        def prep_chunk(cs, zeta):
            nonlocal ci
            pt = kb.next_ps16()
            kb.tr(pt, pt.ap[:, 0:128], k16, k16.ap[:, cs], idt)
            kz_ = kz[ci % 2]
            kb.ts("dve", kz_, pt, zeta.ap[:, 0:1], None, ALU.mult, a_ap=pt.ap[:, 0:128], extra_reads=[zeta])
            pv = kb.next_ps16()
            for vi in range(2):
                kb.tr(pv, pv.ap[:, vi * 128:(vi + 1) * 128], v16, v16.ap[:, vi, cs], idt)
            vt_ = vtm[ci % 2]
            kb.cp("act", vt_, pv, in_ap=pv.ap[:, 0:256])
            ci += 1
            return kz_, vt_

        def state_update(kz_, vt_, gi):
            ps_d = kb.next_ps()
            nxt = Sbfs[1 - sbi[0]]
            kb.mm(ps_d, kz_.ap, vt_.ap, [kz_, vt_], True, True, out_ap=ps_d.ap[:, 0:256])
            kb.stt("dve", nxt, S32, gch.ap[:, gi:gi + 1], ps_d, ALU.mult, ALU.add, b_ap=ps_d.ap[:, 0:256], extra_reads=[gch])
            kb.stt("dve", S32, S32, gch.ap[:, gi:gi + 1], ps_d, ALU.mult, ALU.add, b_ap=ps_d.ap[:, 0:256], extra_reads=[gch])

        kb.op("dve", lambda: nc.vector.memset(S32.ap, 0.0), writes=[S32])
        kb.op("dve", lambda: nc.vector.memset(Sbfs[sbi[0]].ap, 0.0), writes=[Sbfs[sbi[0]]])
        for it, bi in enumerate(range(NB - 1, -1, -1)):
            cols = slice(bi * 512, (bi + 1) * 512)
            q_ = q32[it % 2]; k_ = k32[it % 2]; v_ = v32[it % 2]
            kb.dma("sp", q_.ap, rsrc("q", cols), writes=[q_])
            kb.dma("sp", k_.ap, rsrc("k", cols), writes=[k_])
            kb.dma("sp", v_.ap, rsrc("v", cols), writes=[v_])
            kb.tt("dve", qx, q_, XiB, ALU.mult)
            kb.cp("act", k16, k_)
            kb.cp("pool", v16, v_)
            ps_o = [kb.next_acc(), kb.next_acc()]
            for c in range(3, -1, -1):
                cs = slice(c * 128, (c + 1) * 128)
                kz_, vt_ = prep_chunk(cs, zB)
                state_update(kz_, vt_, 1)
                Sbf = Sbfs[sbi[0]]
                for vi in range(2):
                    kb.mm(ps_o[vi], Sbf.ap[:, vi * 128:(vi + 1) * 128], qx.ap[:, cs], [Sbf, qx], True, True, out_ap=ps_o[vi].ap[:, cs])
                sbi[0] ^= 1
            for vi in range(2):
                st = cst[vi]
                kb.cp("act" if vi == 0 else "dve", st, ps_o[vi])
                kb.dma("sp", cbv[:, vi, cols], st.ap, reads=[st], writes=[cbT[bi]])
        if mid_hook is not None:
            mid_hook()
        kb.op("dve", lambda: nc.vector.memset(S32.ap, 0.0), reads=[], writes=[S32])
        kb.op("dve", lambda: nc.vector.memset(Sbfs[sbi[0]].ap, 0.0), reads=[], writes=[Sbfs[sbi[0]]])
        for bi in range(NB):
            cols = slice(bi * 512, (bi + 1) * 512)
            q_ = q32[bi % 2]; k_ = k32[bi % 2]; v_ = v32[bi % 2]; g_ = g32[bi % 2]; cb_ = cb32[bi % 2]
            kb.dma("sp", q_.ap, rsrc("q", cols), writes=[q_])
            kb.dma("sp", k_.ap, rsrc("k", cols), writes=[k_])
            kb.dma("sp", v_.ap, rsrc("v", cols), writes=[v_])
            kb.dma("sp", g_.ap, rsrc("g", cols), writes=[g_])
            kb.dma("sp", cb_.ap, cbv[:, :, cols], reads=[cbT[bi]], writes=[cb_])
            kb.tt("dve", qx, q_, XiF, ALU.mult)
            kb.cp("act", q16, q_)
            kb.cp("act", k16, k_)
            kb.cp("pool", v16, v_)
            ps_o = [kb.next_acc(), kb.next_acc()]
            for c in range(4):
                cs = slice(c * 128, (c + 1) * 128)
                kz_, vt_ = prep_chunk(cs, zF)
                state_update(kz_, vt_, 0)
                Sbf = Sbfs[sbi[0]]
                ps_s = kb.next_ps()
                kb.mm(ps_s, k16.ap[:, cs], q16.ap[:, cs], [k16, q16], True, True, out_ap=ps_s.ap[:, 0:128])
                P_ = Pm[c % 2]
                kb.tt("dve", P_, ps_s, DcT, ALU.mult, a_ap=ps_s.ap[:, 0:128])
                for vi in range(2):
                    kb.mm(ps_o[vi], vt_.ap[:, vi * 128:(vi + 1) * 128], P_.ap, [vt_, P_], True, False, out_ap=ps_o[vi].ap[:, cs])
                    kb.mm(ps_o[vi], Sbf.ap[:, vi * 128:(vi + 1) * 128], qx.ap[:, cs], [Sbf, qx], False, True, out_ap=ps_o[vi].ap[:, cs])
                sbi[0] ^= 1
            ps_n = kb.next_ps()
            for vi in range(2):
                kb.tt("dve", yv[vi], ps_o[vi], cb_, ALU.add, b_ap=cb_.ap[:, vi, :])
                kb.act(sqb[vi], yv[vi], AF.Square)
                kb.op("pe", lambda: nc.tensor.matmul(ps_n.ap, lhsT=ones.ap, rhs=sqb[vi].ap, start=(vi == 0), stop=(vi == 1)),
                      reads=[ones, sqb[vi]], writes=[ps_n], inc=True, chk_w=(vi == 0))
            kb.ts("dve", rstd, ps_n, 1.0 / 256, EPS, ALU.mult, ALU.add)
            kb.act(rstd, rstd, AF.Sqrt)
            kb.op("dve", lambda: nc.vector.reciprocal(out=rstd.ap, in_=rstd.ap), reads=[rstd], writes=[rstd])
            for vi in range(2):
                kb.act(sg, g_, AF.Silu, in_ap=g_.ap[:, vi, :])
                kb.stt("dve", yv[vi], yv[vi], rn.ap[:, vi:vi + 1], rstd, ALU.mult, ALU.mult, extra_reads=[rn])
                st = ost[vi]
                kb.tt("dve", st, yv[vi], sg, ALU.mult)
                dst = yret(vi, cols.start, 512) if callable(yret) else yret[vi * 128:(vi + 1) * 128, cols]
                kb.dma("sp", dst, st.ap, reads=[st])
        kb_barrier(kb)


ATT_DIL = (1, 4, 16)


def mixer_attn(kb, q3, k3, v3, rbT, C, buckets, yattn, xdt=F32):
    nc = kb.nc
    SEG = 2048
    NSEG = SEQ // SEG
    with ExitStack() as sub:
        sb = lambda n, s, d: sub.enter_context(nc.sbuf_tensor(uname(n), s, d))
        mk = lambda n, s, d=F32: T(sb("at_" + n, s, d)[:], "at_" + n)
        idt32 = mk("id32", [128, 128]); idt = mk("id", [128, 128], BF16)
        kb.dma("sp", idt32.ap, C["ident"], writes=[idt32])
        kb.cp("dve", idt, idt32)
        ones = mk("ones", [128, 128], BF16)
        kb.op("dve", lambda: nc.vector.memset(ones.ap, 1.0), writes=[ones])
        rb = mk("rb", [128, 3, 32])
        for g in range(3):
            kb.dma("sp", rb.ap[:, g, :], rbT[g:g + 1, :].partition_broadcast(128), writes=[rb])
        idx = mk("idx", [128, 128]); acc = mk("acc", [128, 128]); tmpb = mk("tmpb", [128, 128]); eneg = mk("eneg", [128, 128])
        BT = {}
        for g in range(3):
            for kc in range(2):
                kb.dma("sp", idx.ap, C["idxT"][g, kc], writes=[idx])
                kb.dma("sp", acc.ap, C["bandneg"][kc], writes=[acc])
                kb.dma("sp", eneg.ap, C["edgeneg"][kc], writes=[eneg])
                for b in buckets[g]:
                    kb.ts("dve", tmpb, idx, float(b), rb.ap[:, g, b:b + 1], ALU.is_equal, ALU.mult, extra_reads=[rb])
                    kb.tt("dve", acc, acc, tmpb, ALU.add)
                BT[g, kc, 0] = mk(f"BT{g}{kc}m", [128, 128], BF16)
                BT[g, kc, 1] = mk(f"BT{g}{kc}e", [128, 128], BF16)
                kb.cp("dve", BT[g, kc, 0], acc)
                kb.tt("dve", BT[g, kc, 1], acc, eneg, ALU.add)
        x32 = [mk(f"x32{i}", [128, 4096], xdt) for i in range(2)]
        qd = mk("qd", [128, 2048], BF16); kd = mk("kd", [128, 4096], BF16); vd = mk("vd", [128, 4096], BF16)
        vtm = mk("vtm", [128, 32, 128], BF16)
        num = mk("num", [128, 3, SEG]); den = mk("den", [128, 3, SEG])
        Eb = [mk(f"E{i}", [128, 128], BF16) for i in range(4)]
        yo = [mk(f"yo{i}", [128, SEG], BF16) for i in range(3)]
        xi = 0; ei = 0
        for seg in range(NSEG):
            nat0 = seg * SEG
            for g in range(3):
                d = ATT_DIL[g]
                H = 64 * d
                Wd = SEG // d
                Wk = Wd + 128
                nblk = Wd // 128
                lo = nat0 - H; hi = nat0 + SEG + H
                clo = max(lo, 0); chi = min(hi, SEQ)

                def pieces(sec, t0, t1):
                    if callable(q3):
                        return q3(sec, g, t0, t1)
                    return [(0, (q3, k3, v3)[sec][g, :, t0:t1])]

                def load_halo(sec):
                    nonlocal xi
                    x = x32[xi % 2]; xi += 1
                    if clo > lo:
                        kb.op("pool", lambda: nc.gpsimd.memset(x.ap[:, 0:clo - lo], 0.0), writes=[x])
                    if chi < hi:
                        kb.op("pool", lambda: nc.gpsimd.memset(x.ap[:, chi - lo:hi - lo], 0.0), writes=[x])
                    for off, ap in pieces(sec, clo, chi):
                        n_ = ap.shape[-1]
                        kb.dma("sp", x.ap[:, clo - lo + off:clo - lo + off + n_], ap, writes=[x])
                    return x
                xq = x32[xi % 2]; xi += 1
                for off, ap in pieces(0, nat0, nat0 + SEG):
                    kb.dma("sp", xq.ap[:, off:off + ap.shape[-1]], ap, writes=[xq])
                kb.cp("act", qd, xq, out_ap=qd.ap[:, 0:SEG].rearrange("c (r i) -> c r i", r=d),
                      in_ap=xq.ap[:, 0:SEG].rearrange("c (i r) -> c r i", r=d))
                xk = load_halo(1)
                kb.cp("dve", kd, xk, out_ap=kd.ap[:, 0:d * Wk].rearrange("c (r i) -> c r i", r=d),
                      in_ap=xk.ap[:, 0:d * Wk].rearrange("c (i r) -> c r i", r=d))
                xv = load_halo(2)
                kb.cp("pool", vd, xv, out_ap=vd.ap[:, 0:d * Wk].rearrange("c (r i) -> c r i", r=d),
                      in_ap=xv.ap[:, 0:d * Wk].rearrange("c (i r) -> c r i", r=d))
                ntile = nblk + 1
                for r in range(d):
                    for m in range(0, ntile, 4):
                        mm_ = min(4, ntile - m)
                        pv = kb.next_ps16()
                        for u in range(mm_):
                            kb.tr(pv, pv.ap[:, u * 128:(u + 1) * 128], vd, vd.ap[:, r * Wk + (m + u) * 128:r * Wk + (m + u + 1) * 128], idt)
                        ti0 = r * ntile + m
                        kb.cp("act" if (r + m) % 2 == 0 else "dve", vtm, pv, out_ap=vtm.ap[:, ti0:ti0 + mm_, :],
                              in_ap=pv.ap[:, 0:mm_ * 128].rearrange("p (u c) -> p u c", c=128))
                pairs = [(r, blk) for r in range(d) for blk in range(nblk)]
                numv = num.ap[:, g, :].rearrange("c (i r) -> c r i", r=d)
                denv = den.ap[:, g, :].rearrange("c (i r) -> c r i", r=d)
                for p0 in range(0, len(pairs), 4):
                    ps_o = kb.next_acc(); ps_d = kb.next_acc()
                    for u in range(4):
                        r, blk = pairs[p0 + u]
                        gblk = seg * nblk + blk
                        osl = slice(u * 128, (u + 1) * 128)
                        for kc in range(2):
                            var = 1 if ((kc == 0 and gblk == 0) or (kc == 1 and gblk == SEQ // d // 128 - 1)) else 0
                            ps_s = kb.next_ps()
                            k0 = r * Wk + blk * 128 + kc * 128
                            q0 = r * Wd + blk * 128
                            kb.mm(ps_s, kd.ap[:, k0:k0 + 128], qd.ap[:, q0:q0 + 128], [kd, qd], True, False, out_ap=ps_s.ap[:, 0:128])
                            kb.mm(ps_s, idt.ap, BT[g, kc, var].ap, [idt, BT[g, kc, var]], False, True, out_ap=ps_s.ap[:, 0:128])
                            E = Eb[ei % 4]; ei += 1
                            kb.act(E, ps_s, AF.Exp, in_ap=ps_s.ap[:, 0:128])
                            ti = r * ntile + blk + kc
                            kb.mm(ps_o, vtm.ap[:, ti, :], E.ap, [vtm, E], kc == 0, kc == 1, out_ap=ps_o.ap[:, osl])
                            kb.mm(ps_d, ones.ap, E.ap, [ones, E], kc == 0, kc == 1, out_ap=ps_d.ap[:, osl])
                    if d == 16:
                        r0 = pairs[p0][0]
                        o_n = numv[:, r0:r0 + 4, 0:128]; o_d = denv[:, r0:r0 + 4, 0:128]
                        i_n = ps_o.ap.rearrange("p (u q) -> p u q", q=128); i_d = ps_d.ap.rearrange("p (u q) -> p u q", q=128)
                    else:
                        r0, b0 = pairs[p0]
                        o_n = numv[:, r0, b0 * 128:b0 * 128 + 512]; o_d = denv[:, r0, b0 * 128:b0 * 128 + 512]
                        i_n = ps_o.ap; i_d = ps_d.ap
                    kb.cp("act", num, ps_o, out_ap=o_n, in_ap=i_n)
                    kb.cp("dve", den, ps_d, out_ap=o_d, in_ap=i_d)
            kb.tt("dve", den, den, den, ALU.add, out_ap=den.ap[:, 0, :], a_ap=den.ap[:, 0, :], b_ap=den.ap[:, 1, :])
            kb.tt("dve", den, den, den, ALU.add, out_ap=den.ap[:, 0, :], a_ap=den.ap[:, 0, :], b_ap=den.ap[:, 2, :])
            kb.op("dve", lambda: nc.vector.reciprocal(out=den.ap[:, 0, :], in_=den.ap[:, 0, :]), reads=[den], writes=[den])
            for g in range(3):
                kb.tt("dve" if g < 2 else "pool", yo[g], num, den, ALU.mult, a_ap=num.ap[:, g, :], b_ap=den.ap[:, 0, :])
                dst = yattn(g, nat0, SEG) if callable(yattn) else yattn[g * 128:(g + 1) * 128, nat0:nat0 + SEG]
                kb.dma("sp", dst, yo[g].ap, reads=[yo[g]])
        kb_barrier(kb)


RET_CONSTS = [("ident", [128, 128]), ("dpos", [128, 128]), ("dneg", [128, 128]), ("maskF", [128, 128]), ("maskB", [128, 128]),
              ("idx_row", [128, 128]), ("idx_col", [128, 1])]
ATT_CONSTS = [("ident", [128, 128]), ("idxT", [3, 2, 128, 128]), ("bandneg", [2, 128, 128]), ("edgeneg", [2, 128, 128])]


def ret_consts():
    s = np.arange(128, dtype=np.float32)[:, None]
    t = np.arange(128, dtype=np.float32)[None, :]
    f = lambda a: np.ascontiguousarray(np.broadcast_to(a, (128, 128)).astype(np.float32))
    return {"ident": np.eye(128, dtype=np.float32), "dpos": f(np.maximum(t - s, 0)), "dneg": f(np.maximum(s - t, 0)),
            "maskF": f(t >= s), "maskB": f(s > t), "idx_row": f(t), "idx_col": np.arange(128, dtype=np.float32).reshape(128, 1)}


def _t5_bucket(rel):
    half = 16
    exact = 8
    offset = np.where(rel > 0, half, 0)
    n = np.abs(rel)
    large = exact + (np.log(np.maximum(n, 1) / exact) / np.log(1024 / exact) * (half - exact)).astype(np.int32)
    large = np.minimum(large, half - 1)
    return (offset + np.where(n < exact, n, large)).astype(np.int32)


def att_consts():
    w = np.arange(128)[:, None]
    q = np.arange(128)[None, :]
    idxT = np.zeros((3, 2, 128, 128), np.float32)
    bandneg = np.zeros((2, 128, 128), np.float32)
    edgeneg = np.zeros((2, 128, 128), np.float32)
    buckets = []
    for g, d in enumerate(ATT_DIL):
        bs = set()
        for kc in range(2):
            rel = kc * 128 + w - 64 - q
            band = np.abs(rel) <= 64
            b = _t5_bucket(rel * d)
            idxT[g, kc] = np.where(band, b, -1)
            bandneg[kc] = np.where(band, 0.0, BIG_NEG)
            bs |= set(np.unique(b[band]).tolist())
        buckets.append(sorted(bs))
    edgeneg[0] = np.where(w < 64, BIG_NEG, 0.0) + 0 * q
    edgeneg[1] = np.where(w >= 64, BIG_NEG, 0.0) + 0 * q
    return ({"ident": np.eye(128, dtype=np.float32), "idxT": idxT, "bandneg": bandneg, "edgeneg": edgeneg}, buckets)


def build_mixers():
    nc = bass.Bass("TRN2", target_bir_lowering=False)
    di = lambda n, s, d=F32: nc.dram_tensor(n, s, d, kind="ExternalInput").ap()
    do = lambda n, s, d=BF16: nc.dram_tensor(n, s, d, kind="ExternalOutput").ap()
    with ExitStack() as es:
        es.enter_context(nc.allow_low_precision("bf16 matmul operands, fp32 accumulation"))
        kb = KB(nc, es)
        kb.init_psum(2, 2, 4)
        RC = {n: di("rc_" + n, shp) for n, shp in RET_CONSTS}
        AC = {n: di("ac_" + n, shp) for n, shp in ATT_CONSTS}
        crossb = nc.dram_tensor("crossb", [256, SEQ], F32, kind="Internal").ap()
        mixer_attn(kb, di("q3", [3, 128, SEQ]), di("k3", [3, 128, SEQ]), di("v3", [3, 128, SEQ]), di("rbT", [3, 32]),
                   AC, att_consts()[1], do("yattn", [384, SEQ]))
        mixer_ret(kb, di("qT", [128, SEQ]), di("kT", [128, SEQ]), di("vT", [256, SEQ]), di("gT", [256, SEQ]),
                  di("dlog", [1, 2]), di("rnorm", [128, 2]), RC, crossb, do("yret", [256, SEQ]))
        mixer_pool(kb, di("xp_h", [1024, TPC + 2 * PH]), di("invcnt", [4, TPC]), di("pool_w", [1024, 256]),
                   di("pool_scale", [128, 8]), do("ypool", [1024, TPC]))
        mixer_conv(kb, di("cin_h", [2048, TPC + 2 * CH]), di("conv_w", [128, 8, 31]), di("conv_b", [128, 8]),
                   di("cnorm_g", [128, 8]), di("cnorm_b", [128, 8]), AC["ident"], do("yconv", [1024, TPC]))
        kb.finish("sp")
    return nc


XR = 7680
YR = 640
ER = 3072
M_PARAMS = [("rbT", [3, 32]), ("dlog", [1, 2]), ("rnorm", [128, 2]), ("pool_w", [1024, 256]), ("pool_scale", [128, 8]),
            ("conv_w", [128, 8, 31]), ("conv_b", [128, 8]), ("cnorm_g", [128, 8]), ("cnorm_b", [128, 8])]
GROUPS = [[0, 1, 2, 3], [4, 5, 6, 7]]


def build_fused(debug=None):
    nc = bass.Bass("TRN2", target_bir_lowering=False)
    di = lambda n, s, d=F32: nc.dram_tensor(n, s, d, kind="ExternalInput").ap()
    dint = lambda n, s, d=F32: nc.dram_tensor(n, s, d, kind="Internal").ap()
    PAs, PCs, Ms = [], [], []
    if debug is None:
        hin = di("hin", [D, TPC])
        hout = nc.dram_tensor("hout", [D, TPC], F32, kind="ExternalOutput").ap()
        for L in range(DEPTH):
            PAs.append({n: di(f"a{L}_{n}", shp) for n, shp in A_PARAMS})
            PCs.append({n: di(f"c{L}_{n}", shp) for n, shp in C_PARAMS})
    if debug != "xchg":
        for L in range(DEPTH if debug is None else 1):
            Ms.append({n: di(f"m{L}_{n}", shp) for n, shp in M_PARAMS})
        RC = {n: di("rc_" + n, shp) for n, shp in RET_CONSTS}
        AC = {n: di("ac_" + n, shp) for n, shp in ATT_CONSTS}
        invcnt = di("invcnt", [4, TPC])
    XS = [0, 4608, 5632, 6656, XR]
    XDT = [BF16, BF16, BF16, F32]
    XCR = [128, 128, 128, 64]
    xins = [dint(f"xin{k}", [XS[k + 1] - XS[k], TPC], XDT[k]) for k in range(4)]
    xouts = [dint(f"xout{k}", [4 * (XS[k + 1] - XS[k]), TPC], XDT[k]) for k in range(4)]
    ploc = dint("ploc", [ER, TPC])
    mines = [dint("mine0", [4, 9, 128, TPC], BF16), dint("mine1", [4, 2, 128, TPC], BF16),
             dint("mine2", [4, 1, 256, TPC], BF16), dint("mine3", [4, 1, 256, TPC])]
    eL = dint("eL", [ER, 32]); eR = dint("eR", [ER, 32])
    ymine = dint("ymine", [4 * YR, TPC], BF16)

    def xsel(R):
        k = max(i for i in range(4) if R >= XS[i])
        return k, R - XS[k], XS[k + 1] - XS[k]
    ein = dint("ein", [ER, 32]); eout = dint("eout", [4 * ER, 32]); epad = dint("epad", [6 * ER, 32])
    h1T = dint("h1T", [D, TPC])
    yin = dint("yin", [4 * YR, TPC], BF16); yout = dint("yout", [16 * YR, TPC], BF16)
    ypool_l = dint("ypool_l", [1024, TPC], BF16); yconv_l = dint("yconv_l", [1024, TPC], BF16)
    crossb = dint("crossb", [256, SEQ])
    buckets = att_consts()[1]
    with ExitStack() as es:
        es.enter_context(nc.allow_low_precision("bf16 matmul operands, fp32 accumulation"))
        kb = KB(nc, es)
        jq = nc.partition_id() % 4

        def store_ap(j, cols):
            if j < 8:
                return ploc[j * 128:(j + 1) * 128, cols]
            if j < 68:
                k, lr, nr = xsel((j - 8) * 128)
                X = ((lr % 512) // 128) * (nr // 4) + (lr // 512) * 128 if k < 2 else lr
                return xins[k][X:X + 128, cols]
            return ploc[1024 + (j - 68) * 128:1024 + (j - 67) * 128, cols]

        def qkv(sec, g, t0, t1):
            out = []
            t = t0
            while t < t1:
                r = t // TPC
                e = min(t1, (r + 1) * TPC, t + 2048)
                k, lr, nr = xsel(sec * 1536 + 4 * g * 128)
                out.append((t - t0, mines[k][r, lr // 512, :, t - r * TPC:e - r * TPC]))
                t = e
            return out

        def rsrc(nm, cols):
            r = cols.start // TPC
            c0 = cols.start - r * TPC
            c1 = c0 + (cols.stop - cols.start)
            if nm in ("q", "k"):
                k, lr, nr = xsel(4608 if nm == "q" else 5120)
                return mines[k][r, lr // 512, :, c0:c1]
            k = 2 if nm == "v" else 3
            return mines[k][r, 0, :, c0:c1].rearrange("(v p) n -> p v n", p=128)

        def xload(x, row0, t0, t1):
            off = 0
            if t0 < 0:
                n = -t0
                kb.dma("sp", x.ap[:, 0:n], eL[row0:row0 + 128, 32 - n:32], writes=[x])
                off = n
                t0 = 0
            e = min(t1, TPC)
            kb.dma("sp", x.ap[:, off:off + e - t0], ploc[row0:row0 + 128, t0:e], writes=[x])
            off += e - t0
            if t1 > TPC:
                n = t1 - TPC
                kb.dma("sp", x.ap[:, off:off + n], eR[row0:row0 + 128, 0:n], writes=[x])

        def yload(yn, k, t):
            c0 = t * NT
            if yn == "ypool":
                return ypool_l[k * 128:(k + 1) * 128, c0:c0 + NT]
            if yn == "yconv":
                return yconv_l[k * 128:(k + 1) * 128, c0:c0 + NT]
            if yn == "yattn":
                r0 = (k % 4) * YR + (k // 4) * 128
            else:
                r0 = (k // 2) * YR + 384 + (k % 2) * 128
            return ymine[r0:r0 + 128, c0:c0 + NT]

        def dense_phase(Lc, La):
            with ExitStack() as sub:
                kb.es = sub
                kb.init_psum(8, 0, 0)
                dn = Dense(kb)
                if Lc is not None:
                    PC = dict(PCs[Lc])
                    for n in ("mix_g", "ffn2_g", "ple_g", "b_gate"):
                        PC[n] = dn.const(f"c_sb_{n}", PC[n], list(PC[n].shape))
                    PC["macc"] = [T(kb.sb(f"macc{o}", [128, NT], F32)[:], f"macc{o}") for o in range(4)]
                    mrgb = kb.sb("mrg", [128, KC, NT], BF16)
                    PC["mrg"] = [T(mrgb[:, k, :], f"mrg{k}") for k in range(KC)]
                    PC["p32"] = [T(kb.sb(f"p32_{k}", [128, NT], F32)[:], f"p32_{k}") for k in range(2)]
                    PC["yload"] = yload
                if La is not None:
                    PA = dict(PAs[La])
                    for n in ("ffn1_g", "mix_g", "qn", "kn"):
                        PA[n] = dn.const(f"a_sb_{n}", PA[n], list(PA[n].shape))
                    qg = T(kb.sb("qg", [128, 1], F32)[:], "qg")
                    kb.ts("dve", qg, PA["qn"], 128.0 ** -0.5, None, ALU.mult)
                    PA["qg"] = qg
                    PA["kg"] = PA["kn"]
                    PA["cos_t"] = T(kb.sb("cos_t", [128, NT], F32)[:], "cos_t")
                    PA["sin_t"] = T(kb.sb("sin_t", [128, NT], F32)[:], "sin_t")
                src = hin if Lc is None else h1T
                hv = src.rearrange("(kc p) n -> p kc n", p=128)
                for t in range(TPC // NT):
                    cols = slice(t * NT, (t + 1) * NT)
                    kb.dma("sp", dn.hbuf[:], hv[:, :, cols], writes=dn.h)
                    if Lc is not None:
                        phase_C(dn, t, PC)
                    if La is not None:
                        phase_A(dn, t, PA, {"h1T": h1T, "store_ap": store_ap, "is_bf": lambda j: 8 <= j < 60})
                    else:
                        ov = hout.rearrange("(kc p) n -> p kc n", p=128)
                        kb.dma("sp", ov[:, :, cols], dn.hbuf[:], reads=dn.h)
                kb_barrier(kb)
            kb.es = kb.root_es


        def gather_x(ks):
            toks = []
            for k in ks:
                nr = XS[k + 1] - XS[k]
                CR = XCR[k]
                for c in range(nr // CR):
                    toks.append(kb.allgather(xins[k][c * CR:(c + 1) * CR, :], xouts[k][c * 4 * CR:(c + 1) * 4 * CR, :], GROUPS))
            return toks

        def relocate_x(ks):
            for k in ks:
                CR = XCR[k]
                Ck = (XS[k + 1] - XS[k]) // 4 // CR
                v = xouts[k].rearrange("(j c r i) n -> j r c (i n)", j=4, c=Ck, r=4, i=CR)
                dst = mines[k].rearrange("r G p n -> r (G p) n").rearrange("r (c i) n -> r c (i n)", i=CR)
                for r in range(4):
                    kb.dma(("sp", "act", "pool", "sp")[k], dst[r], v[jq, r])

        def mixer_phase(L):
            M = Ms[L]
            for r0 in range(0, ER, 512):
                kb.dma("sp", ein[r0:r0 + 512, 0:16], ploc[r0:r0 + 512, 0:16])
                kb.dma("sp", ein[r0:r0 + 512, 16:32], ploc[r0:r0 + 512, TPC - 16:TPC])
            kb_barrier(kb)
            kb.wait_tokens([kb.allgather(ein, eout, GROUPS)])
            kb.dma("sp", epad[ER:5 * ER, :], eout)
            kb_barrier(kb)
            kb.dma("act", eL, epad.rearrange("(b r) c -> b r c", r=ER)[jq])
            kb.dma("act", eR, epad[2 * ER:6 * ER, :].rearrange("(b r) c -> b r c", r=ER)[jq])
            kb_barrier(kb)
            t_qkv = gather_x((1, 2))
            t_g = gather_x((3,))
            t_att = gather_x((0,))
            with ExitStack() as sub:
                kb.es = sub
                kb.init_psum(2, 2, 4)
                ya = lambda g, t0, n: yin[(t0 // TPC) * YR + g * 128:(t0 // TPC) * YR + (g + 1) * 128, t0 % TPC:t0 % TPC + n]
                yr = lambda vi, t0, n: yin[(t0 // TPC) * YR + 384 + vi * 128:(t0 // TPC) * YR + 384 + (vi + 1) * 128, t0 % TPC:t0 % TPC + n]
                mixer_conv(kb, None, M["conv_w"], M["conv_b"], M["cnorm_g"], M["cnorm_b"], AC["ident"], yconv_l, xload=xload)
                kb.wait_tokens(t_qkv)
                relocate_x((1, 2))
                kb_barrier(kb)

                def mid():
                    kb.wait_tokens(t_g)
                    relocate_x((3,))
                    kb_barrier(kb)
                mixer_ret(kb, rsrc, None, None, None, M["dlog"], M["rnorm"], RC, crossb, yr, xdt=BF16, mid_hook=mid)
                t_y = []
                for c in range(4 * YR // 128):
                    if c % 5 >= 3:
                        t_y.append(kb.allgather(yin[c * 128:(c + 1) * 128, :], yout[c * 512:(c + 1) * 512, :], GROUPS))
                kb.wait_tokens(t_att)
                relocate_x((0,))
                kb_barrier(kb)
                mixer_attn(kb, qkv, None, None, M["rbT"], AC, buckets, ya, xdt=BF16)
                for c in range(4 * YR // 128):
                    if c % 5 < 3:
                        t_y.append(kb.allgather(yin[c * 128:(c + 1) * 128, :], yout[c * 512:(c + 1) * 512, :], GROUPS))
                mixer_pool(kb, None, invcnt, M["pool_w"], M["pool_scale"], ypool_l, xload=xload)
                kb.wait_tokens(t_y)
                kb_barrier(kb)
            kb.es = kb.root_es
            yv = yout.rearrange("(q c r i) n -> q r c (i n)", q=4, c=YR // 128, r=4, i=128)
            yd = ymine.rearrange("(r c i) n -> r c (i n)", r=4, c=YR // 128, i=128)
            for r in range(4):
                kb.dma(("sp", "act")[r % 2], yd[r], yv[jq, r])
            kb_barrier(kb)

        with ExitStack() as sub:
            kb.es = sub
            z = T(kb.sb("zeros", [128, ER // 128, 32], F32)[:], "zeros")
            kb.op("dve", lambda: nc.vector.memset(z.ap, 0.0), writes=[z])
            kb.dma("sp", epad[0:ER, :].rearrange("(k p) c -> p k c", p=128), z.ap, reads=[z])
            kb.dma("sp", epad[5 * ER:6 * ER, :].rearrange("(k p) c -> p k c", p=128), z.ap, reads=[z])
            kb_barrier(kb)
        kb.es = kb.root_es
        if debug == "xchg":
            seed = di("seed", [128, TPC])
            dbg = nc.dram_tensor("dbg", [6, 128, TPC], F32, kind="ExternalOutput").ap()
            for k in range(3, 4):
                nr = XS[k + 1] - XS[k]
                for r0 in range(0, nr, 128):
                    kb.dma("sp", xins[k][r0:r0 + 128, :], seed)
            for r0 in range(0, ER, 128):
                kb.dma("sp", ploc[r0:r0 + 128, :], seed)
            kb_barrier(kb)
            gather_x((3,)); kb.wait_cc(); relocate_x((3,)); kb_barrier(kb)
            kb.dma("sp", dbg[0], mines[3][1, 0, 0:128, :])
            kb.dma("sp", dbg[1], mines[3][3, 0, 128:256, :])
            kb.dma("sp", dbg[2], mines[3][2, 0, 128:256, :])
            kb.dma("sp", dbg[3, :, 0:32], eL[256:384, :])
            kb.dma("sp", dbg[4, :, 0:32], eR[256:384, :])
            kb.dma("sp", dbg[5], xouts[3][0:128, :])
            kb.finish("sp")
            return nc
        if debug == "mix":
            dbg = nc.dram_tensor("dbg", [128, 128], F32, kind="ExternalOutput").ap()
            mixer_phase(0)
            kb.dma("sp", dbg, ploc[0:128, 0:128])
            kb.finish("sp")
            return nc
        dense_phase(None, 0)
        for L in range(DEPTH):
            mixer_phase(L)
            dense_phase(L, L + 1 if L + 1 < DEPTH else None)
        kb.finish("sp")
    return nc


def col_layout(v):
    return np.ascontiguousarray(np.asarray(v, np.float32).reshape(-1, 128).T)


def rope_tables(core):
    pos = ((core % 4) * TPC + np.arange(TPC)).astype(np.float32)
    inv = (10000.0 ** (-np.linspace(0.0, 1.0, 64, dtype=np.float32))).astype(np.float32)
    ang = pos[:, None] * inv[None, :]
    cos = np.cos(ang).astype(np.float32).T
    sin = np.sin(ang).astype(np.float32).T
    return (np.ascontiguousarray(np.concatenate([cos, cos], 0)), np.ascontiguousarray(np.concatenate([-sin, sin], 0)))


def swap_cols(w_in):
    blk = w_in[:, 5632:6656].reshape(w_in.shape[0], 8, 2, 64)
    return np.ascontiguousarray(blk[:, :, ::-1, :].reshape(w_in.shape[0], 1024))


def halo_slice(fm, qt, H):
    s0 = qt * TPC
    out = np.zeros((fm.shape[0], TPC + 2 * H), fm.dtype)
    lo = max(s0 - H, 0)
    hi = min(s0 + TPC + H, SEQ)
    out[:, lo - (s0 - H):hi - (s0 - H)] = fm[:, lo:hi]
    return out


def a_inputs(I, L):
    w_in = I["w_in"][L]
    base = {"a_ffn1_g": col_layout(I["ffn1_norm"][L]), "a_mix_g": col_layout(I["mix_norm"][L]),
            "a_qn": col_layout(I["q_norm"][L]), "a_kn": col_layout(I["k_norm"][L]),
            "a_ffn1_wg": I["ffn1_w_gate"][L], "a_ffn1_wu": I["ffn1_w_up"][L], "a_ffn1_wd": I["ffn1_w_down"][L],
            "a_w_in": w_in, "a_w_in_sw": swap_cols(w_in)}
    per = []
    for c in range(NCORES):
        cosT, sinT = rope_tables(c)
        per.append({"a_cosT": cosT, "a_sinT": sinT})
    return base, per


def c_inputs(I, L):
    base = {"c_mix_g": col_layout(I["mix_norm"][L]), "c_ffn2_g": col_layout(I["ffn2_norm"][L]), "c_ple_g": col_layout(I["ple_norm"][L]),
            "c_b_gate": col_layout(I["b_gate"][L]), "c_w_gate": I["w_gate"][L], "c_w_br_pool": I["w_br_pool"][L],
            "c_w_br_attn": I["w_br_attn"][L], "c_w_br_ret": I["w_br_ret"][L], "c_w_br_conv": I["w_br_conv"][L],
            "c_w_out": I["w_out"][L], "c_ffn2_wg": I["ffn2_w_gate"][L], "c_ffn2_wu": I["ffn2_w_up"][L],
            "c_ffn2_wd": I["ffn2_w_down"][L], "c_w_ple_gate": I["w_ple_gate"][L], "c_w_ple_proj": I["w_ple_proj"][L]}
    per = []
    for c in range(NCORES):
        b, qt = divmod(c, 4)
        per.append({"c_pT": np.ascontiguousarray(I["p"][L, b, qt * TPC:(qt + 1) * TPC].T)})
    return base, per


def mixer_inputs(I, L, projT):
    rc = ret_consts()
    ac, _ = att_consts()
    consts = {"rc_" + k: v for k, v in rc.items()}
    consts.update({"ac_" + k: v for k, v in ac.items()})
    cw = np.asarray(I["conv_w"][L])
    base = {"pool_w": np.ascontiguousarray(np.asarray(I["pool_w"][L]).reshape(1024, 256)), "pool_scale": col_layout(I["pool_scale"][L]),
            "conv_w": np.ascontiguousarray(cw.T.reshape(8, 128, 31).transpose(1, 0, 2)), "conv_b": col_layout(I["conv_b"][L]),
            "cnorm_g": col_layout(I["conv_norm_g"][L]), "cnorm_b": col_layout(I["conv_norm_b"][L])}
    base.update(consts)
    rbT = np.ascontiguousarray(np.asarray(I["rel_bias"]).T)
    per = []
    for c in range(NCORES):
        b, j = divmod(c, 4)
        pb = np.concatenate([projT[b * 4 + q] for q in range(4)], axis=1)
        hs = [j, 4 + j, 8 + j]
        t = (j * TPC + np.arange(TPC))[None, :]
        w = np.array(POOL_WIN)[:, None]
        lo = np.clip(t - w // 2, 0, SEQ)
        hi = np.clip(t - w // 2 + w, 0, SEQ)
        m = {"q3": np.stack([pb[1024 + h * 128:1024 + (h + 1) * 128] for h in hs]),
             "k3": np.stack([pb[2560 + h * 128:2560 + (h + 1) * 128] for h in hs]),
             "v3": np.stack([pb[4096 + h * 128:4096 + (h + 1) * 128] for h in hs]),
             "rbT": np.ascontiguousarray(rbT[hs]),
             "qT": np.ascontiguousarray(pb[5632 + j * 128:5632 + (j + 1) * 128]),
             "kT": np.ascontiguousarray(pb[6144 + j * 128:6144 + (j + 1) * 128]),
             "vT": np.ascontiguousarray(pb[6656 + j * 256:6656 + (j + 1) * 256]),
             "gT": np.ascontiguousarray(pb[7680 + j * 256:7680 + (j + 1) * 256]),
             "dlog": np.ascontiguousarray(np.asarray(I["ret_decay_logit"][L])[:, j].reshape(1, 2)),
             "rnorm": col_layout(np.asarray(I["ret_norm"][L])[j * 256:(j + 1) * 256]),
             "xp_h": halo_slice(pb[0:1024], j, PH), "invcnt": np.full((4, TPC), 1.0, np.float32),
             "cin_h": halo_slice(pb[8704:10752], j, CH)}
        m["invcnt"] = INVCNT[j]
        per.append(m)
        del pb
    return base, per


def _invcnt_tables():
    out = []
    for qt in range(4):
        t = (qt * TPC + np.arange(TPC))[None, :]
        w = np.array(POOL_WIN)[:, None]
        lo = np.clip(t - w // 2, 0, SEQ)
        hi = np.clip(t - w // 2 + w, 0, SEQ)
        cnt = (hi - lo).astype(np.float32)
        tab = np.empty((4, TPC), np.float32)
        for g in range(4):
            tab[g] = np.array([1.0 / c for c in cnt[g]], np.float32)
        out.append(tab)
    return out


INVCNT = _invcnt_tables()
_PROG = []


def kernel(**inputs):
    I = {k: np.asarray(v) for k, v in inputs.items()}
    x = I["x"]
    if not _PROG:
        _PROG.append(build_fused())
    base = {}
    per = [dict() for _ in range(NCORES)]
    rbT = np.ascontiguousarray(I["rel_bias"].T)
    for L in range(DEPTH):
        ab, ap_ = a_inputs(I, L)
        cb, cp_ = c_inputs(I, L)
        base.update({f"a{L}_" + k[2:]: v for k, v in ab.items()})
        base.update({f"c{L}_" + k[2:]: v for k, v in cb.items()})
        cw = I["conv_w"][L]
        base.update({f"m{L}_pool_w": np.ascontiguousarray(I["pool_w"][L].reshape(1024, 256)), f"m{L}_pool_scale": col_layout(I["pool_scale"][L]),
                     f"m{L}_conv_w": np.ascontiguousarray(cw.T.reshape(8, 128, 31).transpose(1, 0, 2)), f"m{L}_conv_b": col_layout(I["conv_b"][L]),
                     f"m{L}_cnorm_g": col_layout(I["conv_norm_g"][L]), f"m{L}_cnorm_b": col_layout(I["conv_norm_b"][L])})
        for c in range(NCORES):
            j = c % 4
            per[c].update({f"a{L}_" + k[2:]: v for k, v in ap_[c].items()})
            per[c].update({f"c{L}_" + k[2:]: v for k, v in cp_[c].items()})
            per[c][f"m{L}_rbT"] = np.ascontiguousarray(rbT[[j, 4 + j, 8 + j]])
            per[c][f"m{L}_dlog"] = np.ascontiguousarray(I["ret_decay_logit"][L][:, j].reshape(1, 2))
            per[c][f"m{L}_rnorm"] = col_layout(I["ret_norm"][L][j * 256:(j + 1) * 256])
    base.update({"rc_" + k: v for k, v in ret_consts().items()})
    base.update({"ac_" + k: v for k, v in att_consts()[0].items()})
    in_maps = []
    for c in range(NCORES):
        m = dict(base)
        m.update(per[c])
        m["invcnt"] = INVCNT[c % 4]
        m["hin"] = np.ascontiguousarray(x[c // 4, (c % 4) * TPC:(c % 4 + 1) * TPC].T)
        in_maps.append(m)
    res = run_bass_kernel_spmd(_PROG[0], in_maps, core_ids=list(range(NCORES)))
    out = np.empty(x.shape, np.float32)
    for c in range(NCORES):
        out[c // 4, (c % 4) * TPC:(c % 4 + 1) * TPC] = res.results[c]["hout"].T
    return out
```

```python
import numpy as np
import ml_dtypes
from contextlib import ExitStack
import concourse.bass as bass
import concourse.mybir as mybir
from concourse.bass_utils import run_bass_kernel_spmd

F32 = mybir.dt.float32
BF16 = mybir.dt.bfloat16
ALU = mybir.AluOpType
AF = mybir.ActivationFunctionType
AX = mybir.AxisListType

NCORES = 8
D = 2048
DFF = 5632
KC = D // 128
FC = DFF // 128
NT = 512
TPC = 4096
SEQ = 16384
DEPTH = 2
EPS = 1e-6
IN_W = 10752
CH_XP = (0, 8); CH_AQ = (8, 20); CH_AK = (20, 32); CH_AV = (32, 44)
CH_RQ = (44, 48); CH_RK = (48, 52); CH_RV = (52, 60); CH_RG = (60, 68); CH_CIN = (68, 84)
EPOCH = 11000


_UID = [0]


def uname(n):
    _UID[0] += 1
    return f"{n}_u{_UID[0]}"


class T:
    __slots__ = ("ap", "lw", "rd", "name")

    def __init__(self, ap, name=""):
        self.ap = ap
        self.lw = None
        self.rd = {}
        self.name = name


class Eng:
    def __init__(self, kb, eng, name):
        self.kb = kb
        self.eng = eng
        self.name = name
        self.sem = kb.new_sem(name + "_s0")
        self.count = 0
        self.nsem = 1
        self.waited = {}
        self.pend_r = []
        self.pend_w = []

    def wait(self, sem, val):
        if self.waited.get(sem, 0) >= val:
            return
        self.eng.wait_ge(sem, val)
        self.waited[sem] = val

    def bump(self, inst):
        if self.count >= EPOCH:
            self.sem = self.kb.new_sem(f"{self.name}_s{self.nsem}")
            self.nsem += 1
            self.count = 0
        self.count += 1
        inst.then_inc(self.sem, 1)
        return (self.sem, self.count)


class KB:
    def __init__(self, nc, es, n_dma_sems=24):
        self.nc = nc
        self.es = es
        self.root_es = es
        self._semn = 0
        self.E = {
            "pe": Eng(self, nc.tensor, "pe"),
            "dve": Eng(self, nc.vector, "dve"),
            "act": Eng(self, nc.scalar, "act"),
            "pool": Eng(self, nc.gpsimd, "pool"),
            "sp": Eng(self, nc.sync, "sp"),
        }
        self.dma_sems = [[self.new_sem(f"dma{i}"), 0] for i in range(n_dma_sems)]
        self.dma_i = 0
        self.cc_sems = [[self.new_sem(f"cc{i}"), 0] for i in range(56)]
        self.cc_i = 0
        self.psb = []
        self.ps_i = 0

    def new_sem(self, name):
        self._semn += 1
        return self.root_es.enter_context(self.nc.semaphore(f"{name}_{self._semn}"))

    def sb(self, name, shape, dt):
        return self.es.enter_context(self.nc.sbuf_tensor(uname(name), shape, dt))

    def ps(self, name, shape, dt=F32):
        return self.es.enter_context(self.nc.psum_tensor(uname(name), shape, dt))

    def init_psum(self, nf32=8, nbf=0, nacc=0):
        self.psb = [T(self.ps(f"ps{i}", [128, 512], F32)[:], f"ps{i}") for i in range(nf32)]
        self.psa = [T(self.ps(f"psa{i}", [128, 512], F32)[:], f"psa{i}") for i in range(nacc)]
        self.psa_i = 0
        self.psb16 = [T(self.ps(f"psh{i}", [128, 1024], BF16)[:], f"psh{i}") for i in range(nbf)]
        self.ps16_i = 0

    def next_ps(self):
        t = self.psb[self.ps_i % len(self.psb)]
        self.ps_i += 1
        return t

    def next_acc(self):
        t = self.psa[self.psa_i % len(self.psa)]
        self.psa_i += 1
        return t

    def next_ps16(self):
        t = self.psb16[self.ps16_i % len(self.psb16)]
        self.ps16_i += 1
        return t

    def tr(self, ps, out_ap, in_t, in_ap, ident_t):
        nc = self.nc
        return self.op("pe", lambda: nc.tensor.transpose(out_ap, in_ap, ident_t.ap), reads=[in_t, ident_t], writes=[ps])

    def cp(self, en, out_t, in_t, out_ap=None, in_ap=None):
        o = out_t.ap if out_ap is None else out_ap
        i = in_t.ap if in_ap is None else in_ap
        if en == "act":
            return self.op("act", lambda: self.nc.scalar.copy(out=o, in_=i), reads=[in_t], writes=[out_t])
        eng = self.nc.vector if en == "dve" else self.nc.gpsimd
        return self.op(en, lambda: eng.tensor_copy(out=o, in_=i), reads=[in_t], writes=[out_t])

    def _deps(self, reads, writes):
        deps = []
        for t in reads:
            if t.lw is not None:
                deps.append(t.lw)
        for t in writes:
            for e in self.E.values():
                if e.pend_r and any(t is p for p in e.pend_r):
                    raise RuntimeError(f"write to {t.name} while un-flushed reads pending on {e.name}")
            if t.lw is not None:
                deps.append(t.lw)
            deps.extend(t.rd.items())
        return deps

    def op(self, en, fn, reads=(), writes=(), inc=True, chk_w=True):
        e = self.E[en]
        deps = self._deps(reads, writes if chk_w else ())
        for sem, val in deps:
            if en == "pe" and sem is e.sem:
                continue
            e.wait(sem, val)
        inst = fn()
        if inc:
            tok = e.bump(inst)
            for t in e.pend_r:
                t.rd[tok[0]] = tok[1]
            for t in reads:
                t.rd[tok[0]] = tok[1]
            for t in e.pend_w:
                t.lw = tok
                t.rd = {}
            for t in writes:
                t.lw = tok
                t.rd = {}
            e.pend_r = []
            e.pend_w = []
        else:
            e.pend_r.extend(reads)
            e.pend_w.extend(writes)
        return inst

    def dma(self, en, out_ap, in_ap, reads=(), writes=(), **kw):
        e = self.E[en]
        for sem, val in self._deps(reads, writes):
            e.wait(sem, val)
        slot = self.dma_sems[self.dma_i % len(self.dma_sems)]
        self.dma_i += 1
        if slot[1] > 0:
            e.wait(slot[0], slot[1])
        inst = e.eng.dma_start(out=out_ap, in_=in_ap, **kw)
        slot[1] += 16
        inst.then_inc(slot[0], 16)
        tok = (slot[0], slot[1])
        for t in writes:
            t.lw = tok
            t.rd = {}
        for t in reads:
            t.rd[tok[0]] = tok[1]
        return tok

    def allgather(self, in_ap, out_ap, groups):
        e = self.E["pool"]
        slot = self.cc_sems[self.cc_i % len(self.cc_sems)]
        self.cc_i += 1
        if slot[1] > 0:
            e.wait(slot[0], slot[1])
        inst = self.nc.gpsimd.collective_compute("AllGather", ALU.bypass, replica_groups=groups, ins=[in_ap], outs=[out_ap])
        slot[1] += 1
        inst.then_inc(slot[0], 1)
        return (slot[0], slot[1])

    def wait_tokens(self, toks):
        for e in self.E.values():
            for sem, val in toks:
                e.wait(sem, val)

    def wait_cc(self):
        for e in self.E.values():
            for sem, val in self.cc_sems:
                if val > 0:
                    e.wait(sem, val)

    def finish(self, en="sp"):
        e = self.E[en]
        for sem, val in self.dma_sems:
            if val > 0:
                e.wait(sem, val)
        for n, o in self.E.items():
            if o is not e and o.count > 0:
                e.wait(o.sem, o.count)

    def mm(self, ps, lhsT_ap, rhs_ap, reads, start, stop, out_ap=None):
        nc = self.nc
        oap = ps.ap if out_ap is None else out_ap
        return self.op("pe", lambda: nc.tensor.matmul(oap, lhsT=lhsT_ap, rhs=rhs_ap, start=start, stop=stop),
                       reads=reads, writes=[ps], inc=stop, chk_w=start)

    def act(self, out_t, in_t, func, out_ap=None, in_ap=None, extra_reads=(), **kw):
        nc = self.nc
        o = out_t.ap if out_ap is None else out_ap
        i = in_t.ap if in_ap is None else in_ap
        return self.op("act", lambda: nc.scalar.activation(out=o, in_=i, func=func, **kw),
                       reads=[in_t] + list(extra_reads), writes=[out_t])

    def tt(self, en, out_t, a_t, b_t, op, out_ap=None, a_ap=None, b_ap=None):
        eng = self.nc.vector if en == "dve" else self.nc.gpsimd
        o = out_t.ap if out_ap is None else out_ap
        a = a_t.ap if a_ap is None else a_ap
        b = b_t.ap if b_ap is None else b_ap
        return self.op(en, lambda: eng.tensor_tensor(out=o, in0=a, in1=b, op=op), reads=[a_t, b_t], writes=[out_t])

    def ts(self, en, out_t, a_t, s1, s2, op0, op1=None, out_ap=None, a_ap=None, extra_reads=()):
        eng = self.nc.vector if en == "dve" else self.nc.gpsimd
        o = out_t.ap if out_ap is None else out_ap
        a = a_t.ap if a_ap is None else a_ap
        if op1 is None:
            f = lambda: eng.tensor_scalar(out=o, in0=a, scalar1=s1, scalar2=None, op0=op0)
        else:
            f = lambda: eng.tensor_scalar(out=o, in0=a, scalar1=s1, scalar2=s2, op0=op0, op1=op1)
        return self.op(en, f, reads=[a_t] + list(extra_reads), writes=[out_t])

    def stt(self, en, out_t, a_t, scalar, b_t, op0, op1, out_ap=None, a_ap=None, b_ap=None, extra_reads=()):
        eng = self.nc.vector if en == "dve" else self.nc.gpsimd
        o = out_t.ap if out_ap is None else out_ap
        a = a_t.ap if a_ap is None else a_ap
        b = b_t.ap if b_ap is None else b_ap
        return self.op(en, lambda: eng.scalar_tensor_tensor(out=o, in0=a, scalar=scalar, in1=b, op0=op0, op1=op1),
                       reads=[a_t, b_t] + list(extra_reads), writes=[out_t])


class Dense:
    def __init__(self, kb):
        self.kb = kb
        nc = kb.nc
        self.hbuf = kb.sb("hbuf", [128, KC, NT], F32)
        self.h = [T(self.hbuf[:, k, :], f"h{k}") for k in range(KC)]
        xnb = kb.sb("xn", [128, KC, NT], BF16)
        self.xn = [T(xnb[:, k, :], f"xn{k}") for k in range(KC)]
        hidb = kb.sb("hid", [128, FC, NT], BF16)
        self.hid = [T(hidb[:, k, :], f"hid{k}") for k in range(FC)]
        self.wsl = [T(kb.sb(f"w{i}", [128, KC, 512], BF16)[:], f"w{i}") for i in range(3)]
        self.wi = 0
        self.stg = [T(kb.sb(f"stg{i}", [128, NT], F32)[:], f"stg{i}") for i in range(4)]
        self.si = 0
        self.stgb = [T(kb.sb(f"stgb{i}", [128, NT], BF16)[:], f"stgb{i}") for i in range(4)]
        self.sbi = 0
        self.tmp = [T(kb.sb(f"tmp{i}", [128, NT], F32)[:], f"tmp{i}") for i in range(4)]
        self.ti = 0
        self.sqb = [T(kb.sb(f"sqb{i}", [128, NT], BF16)[:], f"sqb{i}") for i in range(2)]
        self.qi = 0
        self.rstd = T(kb.sb("rstd", [128, NT], F32)[:], "rstd")
        onesb = kb.sb("ones", [128, 128], BF16)
        self.ones = T(onesb[:], "ones")
        kb.op("dve", lambda: nc.vector.memset(onesb[:], 1.0), writes=[self.ones])

    def wslot(self):
        t = self.wsl[self.wi % len(self.wsl)]
        self.wi += 1
        return t

    def stage(self, bf=False):
        if bf:
            t = self.stgb[self.sbi % len(self.stgb)]
            self.sbi += 1
            return t
        t = self.stg[self.si % len(self.stg)]
        self.si += 1
        return t

    def temp(self):
        t = self.tmp[self.ti % len(self.tmp)]
        self.ti += 1
        return t

    def sq(self):
        t = self.sqb[self.qi % len(self.sqb)]
        self.qi += 1
        return t

    def const(self, name, dram_ap, shape, dt=F32):
        t = T(self.kb.sb(name, shape, dt)[:], name)
        self.kb.dma("sp", t.ap, dram_ap, writes=[t])
        return t

    def rms_rstd(self, chunks, dim, out_rstd, in_aps=None):
        kb = self.kb
        ps = kb.next_ps()
        n = len(chunks)
        for k in range(n):
            sq = self.sq()
            kb.act(sq, chunks[k], AF.Square, in_ap=None if in_aps is None else in_aps[k])
            kb.op("pe", lambda: kb.nc.tensor.matmul(ps.ap, lhsT=self.ones.ap, rhs=sq.ap, start=(k == 0), stop=(k == n - 1)),
                  reads=[self.ones, sq], writes=[ps], inc=True, chk_w=(k == 0))
        kb.ts("dve", out_rstd, ps, 1.0 / dim, EPS, ALU.mult, ALU.add)
        kb.act(out_rstd, out_rstd, AF.Sqrt)
        kb.op("dve", lambda: kb.nc.vector.reciprocal(out=out_rstd.ap, in_=out_rstd.ap), reads=[out_rstd], writes=[out_rstd])

    def rmsnorm_xn(self, gain):
        kb = self.kb
        self.rms_rstd(self.h, D, self.rstd)
        for k in range(KC):
            kb.stt("dve", self.xn[k], self.h[k], gain.ap[:, k:k + 1], self.rstd, ALU.mult, ALU.mult, extra_reads=[gain])

    def linear(self, srcs, nchunks, evac):
        kb = self.kb
        views = [(xc, w.rearrange("(kc p) n -> p kc n", p=128)) for xc, w in srcs]
        for nb in range((nchunks + 3) // 4):
            ncb = min(4, nchunks - nb * 4)
            slots = []
            for xc, wv in views:
                s = self.wslot()
                kk = len(xc)
                kb.dma("pool", s.ap[:, 0:kk, 0:ncb * 128], wv[:, :, nb * 512:nb * 512 + ncb * 128], writes=[s])
                slots.append(s)
            for c in range(ncb):
                pss = []
                for (xc, wv), s in zip(views, slots):
                    ps = kb.next_ps()
                    kk = len(xc)
                    for k in range(kk):
                        kb.mm(ps, s.ap[:, k, c * 128:(c + 1) * 128], xc[k].ap, [s, xc[k]], k == 0, k == kk - 1)
                    pss.append(ps)
                evac(nb * 4 + c, pss)

    def ffn(self, gain, wg, wu, wd):
        kb = self.kb
        nc = kb.nc
        self.rmsnorm_xn(gain)
        hid = self.hid

        def ev(j, pss):
            kb.act(hid[j], pss[0], AF.Silu)
            kb.tt("dve", hid[j], hid[j], pss[1], ALU.mult)
        self.linear([(self.xn, wg), (self.xn, wu)], FC, ev)
        wdv = wd.rearrange("(kc p) n -> p kc n", p=128)
        for og in range(4):
            pos = [kb.next_ps() for _ in range(4)]
            for kg in range(4):
                sd = self.wslot()
                kb.dma("pool", sd.ap[:, 0:11, :], wdv[:, kg * 11:(kg + 1) * 11, og * 512:(og + 1) * 512], writes=[sd])
                for c in range(4):
                    for kk in range(11):
                        k = kg * 11 + kk
                        kb.op("pe", lambda: nc.tensor.matmul(pos[c].ap, lhsT=sd.ap[:, kk, c * 128:(c + 1) * 128], rhs=hid[k].ap,
                                                             start=(k == 0), stop=(k == FC - 1)),
                              reads=[sd, hid[k]], writes=[pos[c]], inc=(k == FC - 1 or kk == 10), chk_w=(k == 0))
            for c in range(4):
                o = og * 4 + c
                kb.stt("dve", self.h[o], pos[c], 0.5, self.h[o], ALU.mult, ALU.add)


def phase_A(dn, t, P, outs):
    kb = dn.kb
    nc = kb.nc
    cols = slice(t * NT, (t + 1) * NT)
    dn.ffn(P["ffn1_g"], P["ffn1_wg"], P["ffn1_wu"], P["ffn1_wd"])
    h1v = outs["h1T"].rearrange("(kc p) n -> p kc n", p=128)
    kb.dma("sp", h1v[:, :, cols], dn.hbuf[:], reads=dn.h)
    store_ap = outs.get("store_ap")
    is_bf = outs.get("is_bf", lambda j: False)
    dn.rmsnorm_xn(P["mix_g"])
    projT = outs.get("projT")
    w_in = P["w_in"]
    cos_t = P["cos_t"]; sin_t = P["sin_t"]
    kb.dma("sp", cos_t.ap, P["cosT"][:, cols], writes=[cos_t])
    kb.dma("sp", sin_t.ap, P["sinT"][:, cols], writes=[sin_t])

    def store(j, st):
        dst = projT[j * 128:(j + 1) * 128, cols] if store_ap is None else store_ap(j, cols)
        kb.dma("sp", dst, st.ap, reads=[st])

    def seg_plain(c0, c1):
        def ev(j, pss):
            st = dn.stage(is_bf(c0 + j))
            if j % 2 == 0:
                kb.act(st, pss[0], AF.Copy)
            else:
                kb.op("dve", lambda: nc.vector.tensor_copy(out=st.ap, in_=pss[0].ap), reads=[pss[0]], writes=[st])
            store(c0 + j, st)
        dn.linear([(dn.xn, w_in[:, c0 * 128:c1 * 128])], c1 - c0, ev)

    def seg_qk(c0, c1, gcol):
        def ev(j, pss):
            rs = dn.temp()
            dn.rms_rstd([pss[0]], 128, rs)
            st = dn.stage(is_bf(c0 + j))
            kb.stt("dve", st, pss[0], gcol.ap[:, 0:1], rs, ALU.mult, ALU.mult, extra_reads=[gcol])
            store(c0 + j, st)
        dn.linear([(dn.xn, w_in[:, c0 * 128:c1 * 128])], c1 - c0, ev)

    def seg_rot(c0, c1, sw0, scale):
        def ev(j, pss):
            t1 = dn.temp()
            kb.stt("dve", t1, pss[0], scale, cos_t, ALU.mult, ALU.mult)
            t2 = dn.temp()
            kb.stt("dve", t2, pss[1], scale, sin_t, ALU.mult, ALU.mult)
            st = dn.stage(is_bf(c0 + j))
            kb.tt("dve", st, t2, t1, ALU.add)
            store(c0 + j, st)
        dn.linear([(dn.xn, w_in[:, c0 * 128:c1 * 128]), (dn.xn, P["w_in_sw"][:, sw0 * 128:(sw0 + c1 - c0) * 128])], c1 - c0, ev)

    seg_plain(*CH_XP)
    seg_qk(CH_AQ[0], CH_AQ[1], P["qg"])
    seg_qk(CH_AK[0], CH_AK[1], P["kg"])
    seg_plain(*CH_AV)
    seg_rot(CH_RQ[0], CH_RQ[1], 0, 1.0)
    seg_rot(CH_RK[0], CH_RK[1], 4, 128.0 ** -0.5)
    seg_plain(CH_RV[0], CH_CIN[1])


def phase_C(dn, t, P):
    kb = dn.kb
    nc = kb.nc
    cols = slice(t * NT, (t + 1) * NT)
    dn.rmsnorm_xn(P["mix_g"])
    merged = P["mrg"]
    macc = P["macc"]
    brs = [("ypool", "w_br_pool", 0, 8), ("yattn", "w_br_attn", 8, 12), ("yret", "w_br_ret", 20, 8), ("yconv", "w_br_conv", 28, 8)]
    yload = P.get("yload")
    for (yn, wn, h0, kk) in brs:
        for k in range(kk):
            src = P[yn][k * 128:(k + 1) * 128, cols] if yload is None else yload(yn, k, t)
            kb.dma("sp", dn.hid[h0 + k].ap, src, writes=[dn.hid[h0 + k]])
    for nb in range(4):
        for i, (yn, wn, h0, kk) in enumerate(brs):
            ych = dn.hid[h0:h0 + kk]

            def ev(c, pss, i=i, nb=nb):
                o = nb * 4 + c
                g = dn.temp()
                kb.act(g, pss[0], AF.Sigmoid, bias=P["b_gate"].ap[:, i * 16 + o:i * 16 + o + 1], extra_reads=[P["b_gate"]])
                if i == 0:
                    kb.tt("dve", macc[c], g, pss[1], ALU.mult)
                else:
                    kb.tt("dve", g, g, pss[1], ALU.mult)
                    if i < 3:
                        kb.tt("dve", macc[c], macc[c], g, ALU.add)
                    else:
                        kb.tt("dve", merged[o], macc[c], g, ALU.add)
            dn.linear([(dn.xn, P["w_gate"][:, i * D + nb * 512:i * D + (nb + 1) * 512]),
                       (ych, P[wn][:, nb * 512:(nb + 1) * 512])], 4, ev)

    def ev_out(o, pss):
        kb.tt("dve", dn.h[o], dn.h[o], pss[0], ALU.add)
    dn.linear([(merged, P["w_out"])], KC, ev_out)
    dn.ffn(P["ffn2_g"], P["ffn2_wg"], P["ffn2_wu"], P["ffn2_wd"])
    dn.rmsnorm_xn(P["ple_g"])
    p32 = P["p32"]; pbf = dn.hid[0:2]
    pv = P["pT"].rearrange("(kc p) n -> p kc n", p=128)
    for k in range(2):
        kb.dma("sp", p32[k].ap, pv[:, k, cols], writes=[p32[k]])
        kb.act(pbf[k], p32[k], AF.Copy)

    def ev_ple(o, pss):
        g = dn.temp()
        kb.act(g, pss[0], AF.Sigmoid)
        kb.tt("dve", g, g, pss[1], ALU.mult)
        kb.tt("dve", dn.h[o], dn.h[o], g, ALU.add)
    dn.linear([(dn.xn, P["w_ple_gate"]), (pbf, P["w_ple_proj"])], KC, ev_ple)


A_PARAMS = [("ffn1_g", [128, KC]), ("mix_g", [128, KC]), ("qn", [128, 1]), ("kn", [128, 1]),
            ("ffn1_wg", [D, DFF]), ("ffn1_wu", [D, DFF]), ("ffn1_wd", [DFF, D]), ("w_in", [D, IN_W]), ("w_in_sw", [D, 1024]),
            ("cosT", [128, TPC]), ("sinT", [128, TPC])]
C_PARAMS = [("mix_g", [128, KC]), ("ffn2_g", [128, KC]), ("ple_g", [128, KC]), ("b_gate", [128, 64]),
            ("w_gate", [D, 4 * D]), ("w_br_pool", [1024, D]), ("w_br_attn", [1536, D]), ("w_br_ret", [1024, D]),
            ("w_br_conv", [1024, D]), ("w_out", [D, D]), ("ffn2_wg", [D, DFF]), ("ffn2_wu", [D, DFF]), ("ffn2_wd", [DFF, D]),
            ("w_ple_gate", [D, D]), ("w_ple_proj", [256, D]), ("pT", [256, TPC])]
C_ACTS = [("ypool", [1024, TPC]), ("yattn", [1536, TPC]), ("yret", [1024, TPC]), ("yconv", [1024, TPC])]
SMALL = {"ffn1_g", "mix_g", "qn", "kn", "ffn2_g", "ple_g", "b_gate"}


def build_dense(doC, doA, ntiles=TPC // NT):
    nc = bass.Bass("TRN2", target_bir_lowering=False)
    hin = nc.dram_tensor("hin", [D, TPC], F32, kind="ExternalInput").ap()
    PA = {}; PC = {}
    if doC:
        for n, shp in C_PARAMS:
            PC[n] = nc.dram_tensor("c_" + n, shp, F32, kind="ExternalInput").ap()
        for n, shp in C_ACTS:
            PC[n] = nc.dram_tensor("c_" + n, shp, BF16, kind="ExternalInput").ap()
    if doA:
        for n, shp in A_PARAMS:
            PA[n] = nc.dram_tensor("a_" + n, shp, F32, kind="ExternalInput").ap()
    outs = {}
    if doA:
        outs["h1T"] = nc.dram_tensor("h1T", [D, TPC], F32, kind="ExternalOutput").ap()
        outs["projT"] = nc.dram_tensor("projT", [IN_W, TPC], F32, kind="ExternalOutput").ap()
    else:
        outs["hout"] = nc.dram_tensor("hout", [D, TPC], F32, kind="ExternalOutput").ap()
    with ExitStack() as es:
        es.enter_context(nc.allow_low_precision("bf16 matmul operands, fp32 accumulation"))
        kb = KB(nc, es)
        kb.init_psum()
        dn = Dense(kb)
        if doC:
            for n in ("mix_g", "ffn2_g", "ple_g", "b_gate"):
                PC[n] = dn.const("c_sb_" + n, PC[n], list(PC[n].shape))
            PC["macc"] = [T(kb.sb(f"macc{o}", [128, NT], F32)[:], f"macc{o}") for o in range(4)]
            mrgb = kb.sb("mrg", [128, KC, NT], BF16)
            PC["mrg"] = [T(mrgb[:, k, :], f"mrg{k}") for k in range(KC)]
            PC["p32"] = [T(kb.sb(f"p32_{k}", [128, NT], F32)[:], f"p32_{k}") for k in range(2)]
        if doA:
            for n in ("ffn1_g", "mix_g", "qn", "kn"):
                PA[n] = dn.const("a_sb_" + n, PA[n], list(PA[n].shape))
            qg = T(kb.sb("qg", [128, 1], F32)[:], "qg")
            kb.ts("dve", qg, PA["qn"], 128.0 ** -0.5, None, ALU.mult)
            PA["qg"] = qg
            PA["cos_t"] = T(kb.sb("cos_t", [128, NT], F32)[:], "cos_t")
            PA["sin_t"] = T(kb.sb("sin_t", [128, NT], F32)[:], "sin_t")
            PA["kg"] = PA["kn"]
        hv = hin.rearrange("(kc p) n -> p kc n", p=128)
        for t in range(ntiles):
            cols = slice(t * NT, (t + 1) * NT)
            kb.dma("sp", dn.hbuf[:], hv[:, :, cols], writes=dn.h)
            if doC:
                phase_C(dn, t, PC)
            if doA:
                phase_A(dn, t, PA, outs)
            else:
                ov = outs["hout"].rearrange("(kc p) n -> p kc n", p=128)
                kb.dma("sp", ov[:, :, cols], dn.hbuf[:], reads=dn.h)
        kb.finish("sp")
    return nc


POOL_WIN = (2, 4, 8, 16)
PH = 8
CH = 15
BIG_NEG = -30000.0


def kb_barrier(kb):
    for e in kb.E.values():
        for sem, val in kb.dma_sems:
            if val > 0:
                e.wait(sem, val)
        for o in kb.E.values():
            if o is not e and o.count > 0:
                e.wait(o.sem, o.count)


def mixer_pool(kb, xp_h, invcnt, pool_w, pool_scale, ypool, xload=None):
    nc = kb.nc
    TT_ = 2048
    W = TT_ + 2 * PH
    with ExitStack() as sub:
        sb = lambda n, s, d: sub.enter_context(nc.sbuf_tensor(uname(n), s, d))
        xb = [T(sb(f"pl_x{i}", [128, W], F32)[:], f"pl_x{i}") for i in range(2)]
        ab = [T(sb(f"pl_a{i}", [128, W], F32)[:], f"pl_a{i}") for i in range(2)]
        ic = T(sb("pl_ic", [128, TT_], F32)[:], "pl_ic")
        mx = [T(sb(f"pl_m{i}", [128, TT_], BF16)[:], f"pl_m{i}") for i in range(2)]
        wsb = T(sb("pl_w", [128, 8, 256], BF16)[:], "pl_w")
        psc = T(sb("pl_sc", [128, 8], F32)[:], "pl_sc")
        stg = [T(sb(f"pl_o{i}", [128, 512], BF16)[:], f"pl_o{i}") for i in range(3)]
        kb.dma("pool", wsb.ap, pool_w.rearrange("(k p) n -> p k n", p=128), writes=[wsb])
        kb.dma("sp", psc.ap, pool_scale, writes=[psc])
        si = 0
        for tt in range(TPC // TT_):
            for g in range(4):
                w = POOL_WIN[g]
                kb.dma("sp", ic.ap, invcnt[g:g + 1, tt * TT_:(tt + 1) * TT_].partition_broadcast(128), writes=[ic])
                for kc in range(2):
                    ch = g * 2 + kc
                    x = xb[kc]
                    if xload is None:
                        kb.dma("sp", x.ap, xp_h[ch * 128:(ch + 1) * 128, tt * TT_:tt * TT_ + W], writes=[x])
                    else:
                        xload(x, ch * 128, tt * TT_ - PH, tt * TT_ + TT_ + PH)
                    cur = x
                    sh = 1
                    pp = 0
                    while sh < w:
                        nxt = ab[pp]; pp ^= 1
                        lo = 2 * sh - 1
                        kb.tt("dve", nxt, cur, cur, ALU.add, out_ap=nxt.ap[:, lo:W], a_ap=cur.ap[:, lo:W], b_ap=cur.ap[:, lo - sh:W - sh])
                        cur = nxt
                        sh *= 2
                    off = PH + w // 2 - 1
                    tmp = ab[pp]
                    kb.tt("dve", tmp, cur, ic, ALU.mult, out_ap=tmp.ap[:, 0:TT_], a_ap=cur.ap[:, off:off + TT_])
                    kb.tt("dve", mx[kc], tmp, x, ALU.subtract, a_ap=tmp.ap[:, 0:TT_], b_ap=x.ap[:, PH:PH + TT_])
                for q in range(TT_ // 512):
                    for dc in range(2):
                        ps = kb.next_ps()
                        for kc in range(2):
                            kb.mm(ps, wsb.ap[:, g * 2 + kc, dc * 128:(dc + 1) * 128], mx[kc].ap[:, q * 512:(q + 1) * 512],
                                  [wsb, mx[kc]], kc == 0, kc == 1)
                        st = stg[si % 3]; si += 1
                        ch = g * 2 + dc
                        kb.ts("dve", st, ps, psc.ap[:, ch:ch + 1], None, ALU.mult, extra_reads=[psc])
                        c0 = tt * TT_ + q * 512
                        kb.dma("sp", ypool[ch * 128:(ch + 1) * 128, c0:c0 + 512], st.ap, reads=[st])
        kb_barrier(kb)


def mixer_conv(kb, cin_h, conv_w, conv_b, norm_g, norm_b, ident, yconv, xload=None):
    nc = kb.nc
    W = 512 + 2 * CH
    with ExitStack() as sub:
        sb = lambda n, s, d: sub.enter_context(nc.sbuf_tensor(uname(n), s, d))
        cw = T(sb("cv_w", [128, 8, 31], F32)[:], "cv_w")
        cb = T(sb("cv_b", [128, 8], F32)[:], "cv_b")
        ng = T(sb("cv_g", [128, 8], F32)[:], "cv_g")
        nb = T(sb("cv_nb", [128, 8], F32)[:], "cv_nb")
        idt = T(sb("cv_id", [128, 128], F32)[:], "cv_id")
        diag = T(sb("cv_diag", [128, 8 * 31, 128], BF16)[:], "cv_diag")
        ones = T(sb("cv_ones", [128, 128], BF16)[:], "cv_ones")
        a32 = [T(sb(f"cv_a{i}", [128, W], F32)[:], f"cv_a{i}") for i in range(2)]
        g32 = [T(sb(f"cv_g32{i}", [128, W], F32)[:], f"cv_g32{i}") for i in range(2)]
        hg = [T(sb(f"cv_h{i}", [128, W], BF16)[:], f"cv_h{i}") for i in range(2)]
        xc = [T(sb(f"cv_x{i}", [128, 512], F32)[:], f"cv_x{i}") for i in range(8)]
        xbf = [T(sb(f"cv_xb{i}", [128, 512], BF16)[:], f"cv_xb{i}") for i in range(2)]
        mean = T(sb("cv_mean", [128, 512], F32)[:], "cv_mean")
        rstd = T(sb("cv_rstd", [128, 512], F32)[:], "cv_rstd")
        msq = T(sb("cv_msq", [128, 512], F32)[:], "cv_msq")
        stg = [T(sb(f"cv_o{i}", [128, 512], BF16)[:], f"cv_o{i}") for i in range(3)]
        for t_, src in ((cw, conv_w), (cb, conv_b), (ng, norm_g), (nb, norm_b), (idt, ident)):
            kb.dma("sp", t_.ap, src, writes=[t_])
        kb.op("dve", lambda: nc.vector.memset(ones.ap, 1.0), writes=[ones])
        for ch in range(8):
            for j in range(31):
                kb.ts("dve", diag, idt, cw.ap[:, ch, j:j + 1], None, ALU.mult,
                      out_ap=diag.ap[:, ch * 31 + j, :], extra_reads=[cw])
        si = 0
        for tt in range(TPC // 512):
            c0 = tt * 512
            for ch in range(8):
                a = a32[ch % 2]; g = g32[ch % 2]; h = hg[ch % 2]
                if xload is None:
                    kb.dma("sp", a.ap, cin_h[ch * 128:(ch + 1) * 128, c0:c0 + W], writes=[a])
                    kb.dma("sp", g.ap, cin_h[1024 + ch * 128:1024 + (ch + 1) * 128, c0:c0 + W], writes=[g])
                else:
                    xload(a, 1024 + ch * 128, c0 - CH, c0 + 512 + CH)
                    xload(g, 2048 + ch * 128, c0 - CH, c0 + 512 + CH)
                kb.act(g, g, AF.Sigmoid)
                kb.tt("dve", h, a, g, ALU.mult)
                ps = kb.next_ps()
                for j in range(31):
                    kb.mm(ps, diag.ap[:, ch * 31 + j, :], h.ap[:, j:j + 512], [diag, h], j == 0, j == 30)
                kb.ts("dve", xc[ch], ps, cb.ap[:, ch:ch + 1], None, ALU.add, extra_reads=[cb])
            ps_m = kb.next_ps()
            ps_q = kb.next_ps()
            for ch in range(8):
                xb_ = xbf[ch % 2]
                kb.act(xb_, xc[ch], AF.Copy)
                kb.op("pe", lambda: nc.tensor.matmul(ps_m.ap, lhsT=ones.ap, rhs=xb_.ap, start=(ch == 0), stop=(ch == 7)),
                      reads=[ones, xb_], writes=[ps_m], inc=True, chk_w=(ch == 0))
            for ch in range(8):
                xb_ = xbf[ch % 2]
                kb.act(xb_, xc[ch], AF.Square)
                kb.op("pe", lambda: nc.tensor.matmul(ps_q.ap, lhsT=ones.ap, rhs=xb_.ap, start=(ch == 0), stop=(ch == 7)),
                      reads=[ones, xb_], writes=[ps_q], inc=True, chk_w=(ch == 0))
            kb.ts("dve", mean, ps_m, 1.0 / 1024, None, ALU.mult)
            kb.tt("dve", msq, mean, mean, ALU.mult)
            kb.stt("dve", rstd, ps_q, 1.0 / 1024, msq, ALU.mult, ALU.subtract)
            kb.ts("dve", rstd, rstd, EPS, None, ALU.add)
            kb.act(rstd, rstd, AF.Sqrt)
            kb.op("dve", lambda: nc.vector.reciprocal(out=rstd.ap, in_=rstd.ap), reads=[rstd], writes=[rstd])
            for ch in range(8):
                kb.tt("dve", xc[ch], xc[ch], mean, ALU.subtract)
                kb.tt("dve", xc[ch], xc[ch], rstd, ALU.mult)
                st = stg[si % 3]; si += 1
                kb.act(st, xc[ch], AF.Silu, scale=ng.ap[:, ch:ch + 1], bias=nb.ap[:, ch:ch + 1], extra_reads=[ng, nb])
                kb.dma("sp", yconv[ch * 128:(ch + 1) * 128, c0:c0 + 512], st.ap, reads=[st])
        kb_barrier(kb)


def mixer_ret(kb, qT, kT, vT, gT, dlog, rnorm, C, crossb, yret, xdt=F32, mid_hook=None):
    nc = kb.nc
    NB = SEQ // 512
    with ExitStack() as sub:
        sb = lambda n, s, d: sub.enter_context(nc.sbuf_tensor(uname(n), s, d))
        mk = lambda n, s, d=F32: T(sb("rt_" + n, s, d)[:], "rt_" + n)
        idt32 = mk("id32", [128, 128]); idt = mk("id", [128, 128], BF16)
        dpos = mk("dpos", [128, 128]); dneg = mk("dneg", [128, 128]); mF = mk("mF", [128, 128]); mB = mk("mB", [128, 128])
        irow = mk("irow", [128, 128]); icol = mk("icol", [128, 1])
        for t_, n in ((idt32, "ident"), (dpos, "dpos"), (dneg, "dneg"), (mF, "maskF"), (mB, "maskB"), (irow, "idx_row"), (icol, "idx_col")):
            kb.dma("sp", t_.ap, C[n], writes=[t_])
        kb.cp("dve", idt, idt32)
        ones = mk("ones", [128, 128], BF16)
        kb.op("dve", lambda: nc.vector.memset(ones.ap, 1.0), writes=[ones])
        rn = mk("rn", [128, 2])
        kb.dma("sp", rn.ap, rnorm, writes=[rn])
        LG = mk("LG", [128, 2]); NL = mk("NL", [128, 2]); LG128 = mk("LG128", [128, 2]); LG127 = mk("LG127", [128, 2]); gch = mk("gch", [128, 2])
        kb.dma("sp", LG.ap, dlog.partition_broadcast(128), writes=[LG])
        kb.act(NL, LG, AF.Exp, scale=-1.0)
        kb.act(NL, NL, AF.Ln, bias=1.0)
        kb.ts("dve", LG, NL, -1.0, None, ALU.mult)
        kb.ts("dve", LG128, LG, 128.0, None, ALU.mult)
        kb.ts("dve", LG127, LG, 127.0, None, ALU.mult)
        kb.act(gch, LG128, AF.Exp)
        DcT = mk("DcT", [128, 128]); etmp = mk("etmp", [128, 128])
        kb.act(DcT, dpos, AF.Exp, scale=LG.ap[:, 0:1], extra_reads=[LG])
        kb.tt("dve", DcT, DcT, mF, ALU.mult)
        kb.act(etmp, dneg, AF.Exp, scale=LG.ap[:, 1:2], extra_reads=[LG])
        kb.tt("dve", etmp, etmp, mB, ALU.mult)
        kb.tt("dve", DcT, DcT, etmp, ALU.add)
        XiF = mk("XiF", [128, 512]); XiB = mk("XiB", [128, 512])
        for c in range(4):
            kb.act(XiF, irow, AF.Exp, out_ap=XiF.ap[:, c * 128:(c + 1) * 128], scale=LG.ap[:, 0:1], bias=LG.ap[:, 0:1], extra_reads=[LG])
            kb.act(XiB, irow, AF.Exp, out_ap=XiB.ap[:, c * 128:(c + 1) * 128], scale=NL.ap[:, 1:2], bias=LG128.ap[:, 1:2], extra_reads=[NL, LG128])
        zF = mk("zF", [128, 1]); zB = mk("zB", [128, 1])
        kb.act(zF, icol, AF.Exp, scale=NL.ap[:, 0:1], bias=LG127.ap[:, 0:1], extra_reads=[NL, LG127])
        kb.act(zB, icol, AF.Exp, scale=LG.ap[:, 1:2], extra_reads=[LG])
        S32 = mk("S32", [128, 256]); Sbf = mk("Sbf", [128, 256], BF16)
        q32 = [mk(f"q32{i}", [128, 512], xdt) for i in range(2)]; k32 = [mk(f"k32{i}", [128, 512], xdt) for i in range(2)]
        v32 = [mk(f"v32{i}", [128, 2, 512], xdt) for i in range(2)]; g32 = [mk(f"g32{i}", [128, 2, 512]) for i in range(2)]
        cb32 = [mk(f"cb32{i}", [128, 2, 512]) for i in range(2)]
        q16 = mk("q16", [128, 512], BF16); qx = mk("qx", [128, 512], BF16); k16 = mk("k16", [128, 512], BF16)
        v16 = mk("v16", [128, 2, 512], BF16)
        kz = [mk(f"kz{i}", [128, 128], BF16) for i in range(2)]; vtm = [mk(f"vtm{i}", [128, 256], BF16) for i in range(2)]
        Pm = [mk(f"P{i}", [128, 128], BF16) for i in range(2)]
        yv = [mk(f"y{i}", [128, 512]) for i in range(2)]; sqb = [mk(f"sq{i}", [128, 512], BF16) for i in range(2)]
        rstd = mk("rstd", [128, 512]); sg = mk("sg", [128, 512])
        ost = [mk(f"ost{i}", [128, 512], BF16) for i in range(2)]; cst = [mk(f"cst{i}", [128, 512]) for i in range(2)]
        if callable(qT):
            rsrc = qT
        else:
            vv = vT.rearrange("(v p) n -> p v n", p=128); gv = gT.rearrange("(v p) n -> p v n", p=128)
            rsrc = lambda nm, cols: {"q": lambda: qT[:, cols], "k": lambda: kT[:, cols], "v": lambda: vv[:, :, cols], "g": lambda: gv[:, :, cols]}[nm]()
        cbv = crossb.rearrange("(v p) n -> p v n", p=128)
        cbT = [T(None, f"cb{bi}") for bi in range(NB)]
        ci = 0

        def prep_chunk(cs, zeta):
            nonlocal ci
            pt = kb.next_ps16()
            kb.tr(pt, pt.ap[:, 0:128], k16, k16.ap[:, cs], idt)
            kz_ = kz[ci % 2]
            kb.ts("dve", kz_, pt, zeta.ap[:, 0:1], None, ALU.mult, a_ap=pt.ap[:, 0:128], extra_reads=[zeta])
            pv = kb.next_ps16()
            for vi in range(2):
                kb.tr(pv, pv.ap[:, vi * 128:(vi + 1) * 128], v16, v16.ap[:, vi, cs], idt)
            vt_ = vtm[ci % 2]
            kb.cp("act", vt_, pv, in_ap=pv.ap[:, 0:256])
            ci += 1
            return kz_, vt_

        def state_update(kz_, vt_, gi):
            ps_d = kb.next_ps()
            kb.mm(ps_d, kz_.ap, vt_.ap, [kz_, vt_], True, True, out_ap=ps_d.ap[:, 0:256])
            kb.stt("dve", Sbf, S32, gch.ap[:, gi:gi + 1], ps_d, ALU.mult, ALU.add, b_ap=ps_d.ap[:, 0:256], extra_reads=[gch])
            kb.stt("dve", S32, S32, gch.ap[:, gi:gi + 1], ps_d, ALU.mult, ALU.add, b_ap=ps_d.ap[:, 0:256], extra_reads=[gch])

        kb.op("dve", lambda: nc.vector.memset(S32.ap, 0.0), writes=[S32])
        kb.op("dve", lambda: nc.vector.memset(Sbf.ap, 0.0), writes=[Sbf])
        for it, bi in enumerate(range(NB - 1, -1, -1)):
            cols = slice(bi * 512, (bi + 1) * 512)
            q_ = q32[it % 2]; k_ = k32[it % 2]; v_ = v32[it % 2]
            kb.dma("sp", q_.ap, rsrc("q", cols), writes=[q_])
            kb.dma("sp", k_.ap, rsrc("k", cols), writes=[k_])
            kb.dma("sp", v_.ap, rsrc("v", cols), writes=[v_])
            kb.tt("dve", qx, q_, XiB, ALU.mult)
            kb.cp("act", k16, k_)
            kb.cp("pool", v16, v_)
            ps_o = [kb.next_acc(), kb.next_acc()]
            for c in range(3, -1, -1):
                cs = slice(c * 128, (c + 1) * 128)
                for vi in range(2):
                    kb.mm(ps_o[vi], Sbf.ap[:, vi * 128:(vi + 1) * 128], qx.ap[:, cs], [Sbf, qx], True, True, out_ap=ps_o[vi].ap[:, cs])
                kz_, vt_ = prep_chunk(cs, zB)
                state_update(kz_, vt_, 1)
            for vi in range(2):
                st = cst[vi]
                kb.cp("act" if vi == 0 else "dve", st, ps_o[vi])
                kb.dma("sp", cbv[:, vi, cols], st.ap, reads=[st], writes=[cbT[bi]])
        if mid_hook is not None:
            mid_hook()
        kb.op("dve", lambda: nc.vector.memset(S32.ap, 0.0), reads=[], writes=[S32])
        kb.op("dve", lambda: nc.vector.memset(Sbf.ap, 0.0), reads=[], writes=[Sbf])
        for bi in range(NB):
            cols = slice(bi * 512, (bi + 1) * 512)
            q_ = q32[bi % 2]; k_ = k32[bi % 2]; v_ = v32[bi % 2]; g_ = g32[bi % 2]; cb_ = cb32[bi % 2]
            kb.dma("sp", q_.ap, rsrc("q", cols), writes=[q_])
            kb.dma("sp", k_.ap, rsrc("k", cols), writes=[k_])
            kb.dma("sp", v_.ap, rsrc("v", cols), writes=[v_])
            kb.dma("sp", g_.ap, rsrc("g", cols), writes=[g_])
            kb.dma("sp", cb_.ap, cbv[:, :, cols], reads=[cbT[bi]], writes=[cb_])
            kb.tt("dve", qx, q_, XiF, ALU.mult)
            kb.cp("act", q16, q_)
            kb.cp("act", k16, k_)
            kb.cp("pool", v16, v_)
            ps_o = [kb.next_acc(), kb.next_acc()]
            for c in range(4):
                cs = slice(c * 128, (c + 1) * 128)
                kz_, vt_ = prep_chunk(cs, zF)
                ps_s = kb.next_ps()
                kb.mm(ps_s, k16.ap[:, cs], q16.ap[:, cs], [k16, q16], True, True, out_ap=ps_s.ap[:, 0:128])
                P_ = Pm[c % 2]
                kb.tt("dve", P_, ps_s, DcT, ALU.mult, a_ap=ps_s.ap[:, 0:128])
                for vi in range(2):
                    kb.mm(ps_o[vi], vt_.ap[:, vi * 128:(vi + 1) * 128], P_.ap, [vt_, P_], True, False, out_ap=ps_o[vi].ap[:, cs])
                    kb.mm(ps_o[vi], Sbf.ap[:, vi * 128:(vi + 1) * 128], qx.ap[:, cs], [Sbf, qx], False, True, out_ap=ps_o[vi].ap[:, cs])
                state_update(kz_, vt_, 0)
            ps_n = kb.next_ps()
            for vi in range(2):
                kb.tt("dve", yv[vi], ps_o[vi], cb_, ALU.add, b_ap=cb_.ap[:, vi, :])
                kb.act(sqb[vi], yv[vi], AF.Square)
                kb.op("pe", lambda: nc.tensor.matmul(ps_n.ap, lhsT=ones.ap, rhs=sqb[vi].ap, start=(vi == 0), stop=(vi == 1)),
                      reads=[ones, sqb[vi]], writes=[ps_n], inc=True, chk_w=(vi == 0))
            kb.ts("dve", rstd, ps_n, 1.0 / 256, EPS, ALU.mult, ALU.add)
            kb.act(rstd, rstd, AF.Sqrt)
            kb.op("dve", lambda: nc.vector.reciprocal(out=rstd.ap, in_=rstd.ap), reads=[rstd], writes=[rstd])
            for vi in range(2):
                kb.act(sg, g_, AF.Silu, in_ap=g_.ap[:, vi, :])
                kb.stt("dve", yv[vi], yv[vi], rn.ap[:, vi:vi + 1], rstd, ALU.mult, ALU.mult, extra_reads=[rn])
                st = ost[vi]
                kb.tt("dve", st, yv[vi], sg, ALU.mult)
                dst = yret(vi, cols.start, 512) if callable(yret) else yret[vi * 128:(vi + 1) * 128, cols]
                kb.dma("sp", dst, st.ap, reads=[st])
        kb_barrier(kb)


ATT_DIL = (1, 4, 16)


def mixer_attn(kb, q3, k3, v3, rbT, C, buckets, yattn, xdt=F32):
    nc = kb.nc
    SEG = 2048
    NSEG = SEQ // SEG
    with ExitStack() as sub:
        sb = lambda n, s, d: sub.enter_context(nc.sbuf_tensor(uname(n), s, d))
        mk = lambda n, s, d=F32: T(sb("at_" + n, s, d)[:], "at_" + n)
        idt32 = mk("id32", [128, 128]); idt = mk("id", [128, 128], BF16)
        kb.dma("sp", idt32.ap, C["ident"], writes=[idt32])
        kb.cp("dve", idt, idt32)
        ones = mk("ones", [128, 128], BF16)
        kb.op("dve", lambda: nc.vector.memset(ones.ap, 1.0), writes=[ones])
        rb = mk("rb", [128, 3, 32])
        for g in range(3):
            kb.dma("sp", rb.ap[:, g, :], rbT[g:g + 1, :].partition_broadcast(128), writes=[rb])
        idx = mk("idx", [128, 128]); acc = mk("acc", [128, 128]); tmpb = mk("tmpb", [128, 128]); eneg = mk("eneg", [128, 128])
        BT = {}
        for g in range(3):
            for kc in range(2):
                kb.dma("sp", idx.ap, C["idxT"][g, kc], writes=[idx])
                kb.dma("sp", acc.ap, C["bandneg"][kc], writes=[acc])
                kb.dma("sp", eneg.ap, C["edgeneg"][kc], writes=[eneg])
                for b in buckets[g]:
                    kb.ts("dve", tmpb, idx, float(b), rb.ap[:, g, b:b + 1], ALU.is_equal, ALU.mult, extra_reads=[rb])
                    kb.tt("dve", acc, acc, tmpb, ALU.add)
                BT[g, kc, 0] = mk(f"BT{g}{kc}m", [128, 128], BF16)
                BT[g, kc, 1] = mk(f"BT{g}{kc}e", [128, 128], BF16)
                kb.cp("dve", BT[g, kc, 0], acc)
                kb.tt("dve", BT[g, kc, 1], acc, eneg, ALU.add)
        x32 = [mk(f"x32{i}", [128, 4096], xdt) for i in range(2)]
        qd = mk("qd", [128, 2048], BF16); kd = mk("kd", [128, 4096], BF16); vd = mk("vd", [128, 4096], BF16)
        vtm = mk("vtm", [128, 32, 128], BF16)
        num = mk("num", [128, 3, SEG]); den = mk("den", [128, 3, SEG])
        Eb = [mk(f"E{i}", [128, 128], BF16) for i in range(4)]
        yo = [mk(f"yo{i}", [128, SEG], BF16) for i in range(3)]
        xi = 0; ei = 0
        for seg in range(NSEG):
            nat0 = seg * SEG
            for g in range(3):
                d = ATT_DIL[g]
                H = 64 * d
                Wd = SEG // d
                Wk = Wd + 128
                nblk = Wd // 128
                lo = nat0 - H; hi = nat0 + SEG + H
                clo = max(lo, 0); chi = min(hi, SEQ)

                def pieces(sec, t0, t1):
                    if callable(q3):
                        return q3(sec, g, t0, t1)
                    return [(0, (q3, k3, v3)[sec][g, :, t0:t1])]

                def load_halo(sec):
                    nonlocal xi
                    x = x32[xi % 2]; xi += 1
                    if clo > lo:
                        kb.op("pool", lambda: nc.gpsimd.memset(x.ap[:, 0:clo - lo], 0.0), writes=[x])
                    if chi < hi:
                        kb.op("pool", lambda: nc.gpsimd.memset(x.ap[:, chi - lo:hi - lo], 0.0), writes=[x])
                    for off, ap in pieces(sec, clo, chi):
                        n_ = ap.shape[-1]
                        kb.dma("sp", x.ap[:, clo - lo + off:clo - lo + off + n_], ap, writes=[x])
                    return x
                xq = x32[xi % 2]; xi += 1
                for off, ap in pieces(0, nat0, nat0 + SEG):
                    kb.dma("sp", xq.ap[:, off:off + ap.shape[-1]], ap, writes=[xq])
                kb.cp("act", qd, xq, out_ap=qd.ap[:, 0:SEG].rearrange("c (r i) -> c r i", r=d),
                      in_ap=xq.ap[:, 0:SEG].rearrange("c (i r) -> c r i", r=d))
                xk = load_halo(1)
                kb.cp("dve", kd, xk, out_ap=kd.ap[:, 0:d * Wk].rearrange("c (r i) -> c r i", r=d),
                      in_ap=xk.ap[:, 0:d * Wk].rearrange("c (i r) -> c r i", r=d))
                xv = load_halo(2)
                kb.cp("pool", vd, xv, out_ap=vd.ap[:, 0:d * Wk].rearrange("c (r i) -> c r i", r=d),
                      in_ap=xv.ap[:, 0:d * Wk].rearrange("c (i r) -> c r i", r=d))
                ntile = nblk + 1
                for r in range(d):
                    for m in range(0, ntile, 4):
                        mm_ = min(4, ntile - m)
                        pv = kb.next_ps16()
                        for u in range(mm_):
                            kb.tr(pv, pv.ap[:, u * 128:(u + 1) * 128], vd, vd.ap[:, r * Wk + (m + u) * 128:r * Wk + (m + u + 1) * 128], idt)
                        ti0 = r * ntile + m
                        kb.cp("act" if (r + m) % 2 == 0 else "dve", vtm, pv, out_ap=vtm.ap[:, ti0:ti0 + mm_, :],
                              in_ap=pv.ap[:, 0:mm_ * 128].rearrange("p (u c) -> p u c", c=128))
                pairs = [(r, blk) for r in range(d) for blk in range(nblk)]
                numv = num.ap[:, g, :].rearrange("c (i r) -> c r i", r=d)
                denv = den.ap[:, g, :].rearrange("c (i r) -> c r i", r=d)
                for p0 in range(0, len(pairs), 4):
                    ps_o = kb.next_acc(); ps_d = kb.next_acc()
                    for u in range(4):
                        r, blk = pairs[p0 + u]
                        gblk = seg * nblk + blk
                        osl = slice(u * 128, (u + 1) * 128)
                        for kc in range(2):
                            var = 1 if ((kc == 0 and gblk == 0) or (kc == 1 and gblk == SEQ // d // 128 - 1)) else 0
                            ps_s = kb.next_ps()
                            k0 = r * Wk + blk * 128 + kc * 128
                            q0 = r * Wd + blk * 128
                            kb.mm(ps_s, kd.ap[:, k0:k0 + 128], qd.ap[:, q0:q0 + 128], [kd, qd], True, False, out_ap=ps_s.ap[:, 0:128])
                            kb.mm(ps_s, idt.ap, BT[g, kc, var].ap, [idt, BT[g, kc, var]], False, True, out_ap=ps_s.ap[:, 0:128])
                            E = Eb[ei % 4]; ei += 1
                            kb.act(E, ps_s, AF.Exp, in_ap=ps_s.ap[:, 0:128])
                            ti = r * ntile + blk + kc
                            kb.mm(ps_o, vtm.ap[:, ti, :], E.ap, [vtm, E], kc == 0, kc == 1, out_ap=ps_o.ap[:, osl])
                            kb.mm(ps_d, ones.ap, E.ap, [ones, E], kc == 0, kc == 1, out_ap=ps_d.ap[:, osl])
                    if d == 16:
                        r0 = pairs[p0][0]
                        o_n = numv[:, r0:r0 + 4, 0:128]; o_d = denv[:, r0:r0 + 4, 0:128]
                        i_n = ps_o.ap.rearrange("p (u q) -> p u q", q=128); i_d = ps_d.ap.rearrange("p (u q) -> p u q", q=128)
                    else:
                        r0, b0 = pairs[p0]
                        o_n = numv[:, r0, b0 * 128:b0 * 128 + 512]; o_d = denv[:, r0, b0 * 128:b0 * 128 + 512]
                        i_n = ps_o.ap; i_d = ps_d.ap
                    kb.cp("act", num, ps_o, out_ap=o_n, in_ap=i_n)
                    kb.cp("dve", den, ps_d, out_ap=o_d, in_ap=i_d)
            kb.tt("dve", den, den, den, ALU.add, out_ap=den.ap[:, 0, :], a_ap=den.ap[:, 0, :], b_ap=den.ap[:, 1, :])
            kb.tt("dve", den, den, den, ALU.add, out_ap=den.ap[:, 0, :], a_ap=den.ap[:, 0, :], b_ap=den.ap[:, 2, :])
            kb.op("dve", lambda: nc.vector.reciprocal(out=den.ap[:, 0, :], in_=den.ap[:, 0, :]), reads=[den], writes=[den])
            for g in range(3):
                kb.tt("dve" if g < 2 else "pool", yo[g], num, den, ALU.mult, a_ap=num.ap[:, g, :], b_ap=den.ap[:, 0, :])
                dst = yattn(g, nat0, SEG) if callable(yattn) else yattn[g * 128:(g + 1) * 128, nat0:nat0 + SEG]
                kb.dma("sp", dst, yo[g].ap, reads=[yo[g]])
        kb_barrier(kb)


RET_CONSTS = [("ident", [128, 128]), ("dpos", [128, 128]), ("dneg", [128, 128]), ("maskF", [128, 128]), ("maskB", [128, 128]),
              ("idx_row", [128, 128]), ("idx_col", [128, 1])]
ATT_CONSTS = [("ident", [128, 128]), ("idxT", [3, 2, 128, 128]), ("bandneg", [2, 128, 128]), ("edgeneg", [2, 128, 128])]


def ret_consts():
    s = np.arange(128, dtype=np.float32)[:, None]
    t = np.arange(128, dtype=np.float32)[None, :]
    f = lambda a: np.ascontiguousarray(np.broadcast_to(a, (128, 128)).astype(np.float32))
    return {"ident": np.eye(128, dtype=np.float32), "dpos": f(np.maximum(t - s, 0)), "dneg": f(np.maximum(s - t, 0)),
            "maskF": f(t >= s), "maskB": f(s > t), "idx_row": f(t), "idx_col": np.arange(128, dtype=np.float32).reshape(128, 1)}


def _t5_bucket(rel):
    half = 16
    exact = 8
    offset = np.where(rel > 0, half, 0)
    n = np.abs(rel)
    large = exact + (np.log(np.maximum(n, 1) / exact) / np.log(1024 / exact) * (half - exact)).astype(np.int32)
    large = np.minimum(large, half - 1)
    return (offset + np.where(n < exact, n, large)).astype(np.int32)


def att_consts():
    w = np.arange(128)[:, None]
    q = np.arange(128)[None, :]
    idxT = np.zeros((3, 2, 128, 128), np.float32)
    bandneg = np.zeros((2, 128, 128), np.float32)
    edgeneg = np.zeros((2, 128, 128), np.float32)
    buckets = []
    for g, d in enumerate(ATT_DIL):
        bs = set()
        for kc in range(2):
            rel = kc * 128 + w - 64 - q
            band = np.abs(rel) <= 64
            b = _t5_bucket(rel * d)
            idxT[g, kc] = np.where(band, b, -1)
            bandneg[kc] = np.where(band, 0.0, BIG_NEG)
            bs |= set(np.unique(b[band]).tolist())
        buckets.append(sorted(bs))
    edgeneg[0] = np.where(w < 64, BIG_NEG, 0.0) + 0 * q
    edgeneg[1] = np.where(w >= 64, BIG_NEG, 0.0) + 0 * q
    return ({"ident": np.eye(128, dtype=np.float32), "idxT": idxT, "bandneg": bandneg, "edgeneg": edgeneg}, buckets)


def build_mixers():
    nc = bass.Bass("TRN2", target_bir_lowering=False)
    di = lambda n, s, d=F32: nc.dram_tensor(n, s, d, kind="ExternalInput").ap()
    do = lambda n, s, d=BF16: nc.dram_tensor(n, s, d, kind="ExternalOutput").ap()
    with ExitStack() as es:
        es.enter_context(nc.allow_low_precision("bf16 matmul operands, fp32 accumulation"))
        kb = KB(nc, es)
        kb.init_psum(2, 2, 4)
        RC = {n: di("rc_" + n, shp) for n, shp in RET_CONSTS}
        AC = {n: di("ac_" + n, shp) for n, shp in ATT_CONSTS}
        crossb = nc.dram_tensor("crossb", [256, SEQ], F32, kind="Internal").ap()
        mixer_attn(kb, di("q3", [3, 128, SEQ]), di("k3", [3, 128, SEQ]), di("v3", [3, 128, SEQ]), di("rbT", [3, 32]),
                   AC, att_consts()[1], do("yattn", [384, SEQ]))
        mixer_ret(kb, di("qT", [128, SEQ]), di("kT", [128, SEQ]), di("vT", [256, SEQ]), di("gT", [256, SEQ]),
                  di("dlog", [1, 2]), di("rnorm", [128, 2]), RC, crossb, do("yret", [256, SEQ]))
        mixer_pool(kb, di("xp_h", [1024, TPC + 2 * PH]), di("invcnt", [4, TPC]), di("pool_w", [1024, 256]),
                   di("pool_scale", [128, 8]), do("ypool", [1024, TPC]))
        mixer_conv(kb, di("cin_h", [2048, TPC + 2 * CH]), di("conv_w", [128, 8, 31]), di("conv_b", [128, 8]),
                   di("cnorm_g", [128, 8]), di("cnorm_b", [128, 8]), AC["ident"], do("yconv", [1024, TPC]))
        kb.finish("sp")
    return nc


XR = 7680
YR = 640
ER = 3072
M_PARAMS = [("rbT", [3, 32]), ("dlog", [1, 2]), ("rnorm", [128, 2]), ("pool_w", [1024, 256]), ("pool_scale", [128, 8]),
            ("conv_w", [128, 8, 31]), ("conv_b", [128, 8]), ("cnorm_g", [128, 8]), ("cnorm_b", [128, 8])]
GROUPS = [[0, 1, 2, 3], [4, 5, 6, 7]]


def build_fused(debug=None):
    nc = bass.Bass("TRN2", target_bir_lowering=False)
    di = lambda n, s, d=F32: nc.dram_tensor(n, s, d, kind="ExternalInput").ap()
    dint = lambda n, s, d=F32: nc.dram_tensor(n, s, d, kind="Internal").ap()
    PAs, PCs, Ms = [], [], []
    if debug is None:
        hin = di("hin", [D, TPC])
        hout = nc.dram_tensor("hout", [D, TPC], F32, kind="ExternalOutput").ap()
        for L in range(DEPTH):
            PAs.append({n: di(f"a{L}_{n}", shp) for n, shp in A_PARAMS})
            PCs.append({n: di(f"c{L}_{n}", shp) for n, shp in C_PARAMS})
    if debug != "xchg":
        for L in range(DEPTH if debug is None else 1):
            Ms.append({n: di(f"m{L}_{n}", shp) for n, shp in M_PARAMS})
        RC = {n: di("rc_" + n, shp) for n, shp in RET_CONSTS}
        AC = {n: di("ac_" + n, shp) for n, shp in ATT_CONSTS}
        invcnt = di("invcnt", [4, TPC])
    XS = [0, 4608, 5632, 6656, XR]
    XDT = [BF16, BF16, BF16, F32]
    XCR = [128, 128, 128, 64]
    xins = [dint(f"xin{k}", [XS[k + 1] - XS[k], TPC], XDT[k]) for k in range(4)]
    xouts = [dint(f"xout{k}", [4 * (XS[k + 1] - XS[k]), TPC], XDT[k]) for k in range(4)]
    ploc = dint("ploc", [ER, TPC])
    mines = [dint("mine0", [4, 9, 128, TPC], BF16), dint("mine1", [4, 2, 128, TPC], BF16),
             dint("mine2", [4, 1, 256, TPC], BF16), dint("mine3", [4, 1, 256, TPC])]
    eL = dint("eL", [ER, 32]); eR = dint("eR", [ER, 32])
    ymine = dint("ymine", [4 * YR, TPC], BF16)

    def xsel(R):
        k = max(i for i in range(4) if R >= XS[i])
        return k, R - XS[k], XS[k + 1] - XS[k]
    ein = dint("ein", [ER, 32]); eout = dint("eout", [4 * ER, 32]); epad = dint("epad", [6 * ER, 32])
    h1T = dint("h1T", [D, TPC])
    yin = dint("yin", [4 * YR, TPC], BF16); yout = dint("yout", [16 * YR, TPC], BF16)
    ypool_l = dint("ypool_l", [1024, TPC], BF16); yconv_l = dint("yconv_l", [1024, TPC], BF16)
    crossb = dint("crossb", [256, SEQ])
    buckets = att_consts()[1]
    with ExitStack() as es:
        es.enter_context(nc.allow_low_precision("bf16 matmul operands, fp32 accumulation"))
        kb = KB(nc, es)
        jq = nc.partition_id() % 4

        def store_ap(j, cols):
            if j < 8:
                return ploc[j * 128:(j + 1) * 128, cols]
            if j < 68:
                k, lr, nr = xsel((j - 8) * 128)
                X = ((lr % 512) // 128) * (nr // 4) + (lr // 512) * 128 if k < 2 else lr
                return xins[k][X:X + 128, cols]
            return ploc[1024 + (j - 68) * 128:1024 + (j - 67) * 128, cols]

        def qkv(sec, g, t0, t1):
            out = []
            t = t0
            while t < t1:
                r = t // TPC
                e = min(t1, (r + 1) * TPC, t + 2048)
                k, lr, nr = xsel(sec * 1536 + 4 * g * 128)
                out.append((t - t0, mines[k][r, lr // 512, :, t - r * TPC:e - r * TPC]))
                t = e
            return out

        def rsrc(nm, cols):
            r = cols.start // TPC
            c0 = cols.start - r * TPC
            c1 = c0 + (cols.stop - cols.start)
            if nm in ("q", "k"):
                k, lr, nr = xsel(4608 if nm == "q" else 5120)
                return mines[k][r, lr // 512, :, c0:c1]
            k = 2 if nm == "v" else 3
            return mines[k][r, 0, :, c0:c1].rearrange("(v p) n -> p v n", p=128)

        def xload(x, row0, t0, t1):
            off = 0
            if t0 < 0:
                n = -t0
                kb.dma("sp", x.ap[:, 0:n], eL[row0:row0 + 128, 32 - n:32], writes=[x])
                off = n
                t0 = 0
            e = min(t1, TPC)
            kb.dma("sp", x.ap[:, off:off + e - t0], ploc[row0:row0 + 128, t0:e], writes=[x])
            off += e - t0
            if t1 > TPC:
                n = t1 - TPC
                kb.dma("sp", x.ap[:, off:off + n], eR[row0:row0 + 128, 0:n], writes=[x])

        def yload(yn, k, t):
            c0 = t * NT
            if yn == "ypool":
                return ypool_l[k * 128:(k + 1) * 128, c0:c0 + NT]
            if yn == "yconv":
                return yconv_l[k * 128:(k + 1) * 128, c0:c0 + NT]
            if yn == "yattn":
                r0 = (k % 4) * YR + (k // 4) * 128
            else:
                r0 = (k // 2) * YR + 384 + (k % 2) * 128
            return ymine[r0:r0 + 128, c0:c0 + NT]

        def dense_phase(Lc, La):
            with ExitStack() as sub:
                kb.es = sub
                kb.init_psum(8, 0, 0)
                dn = Dense(kb)
                if Lc is not None:
                    PC = dict(PCs[Lc])
                    for n in ("mix_g", "ffn2_g", "ple_g", "b_gate"):
                        PC[n] = dn.const(f"c_sb_{n}", PC[n], list(PC[n].shape))
                    PC["macc"] = [T(kb.sb(f"macc{o}", [128, NT], F32)[:], f"macc{o}") for o in range(4)]
                    mrgb = kb.sb("mrg", [128, KC, NT], BF16)
                    PC["mrg"] = [T(mrgb[:, k, :], f"mrg{k}") for k in range(KC)]
                    PC["p32"] = [T(kb.sb(f"p32_{k}", [128, NT], F32)[:], f"p32_{k}") for k in range(2)]
                    PC["yload"] = yload
                if La is not None:
                    PA = dict(PAs[La])
                    for n in ("ffn1_g", "mix_g", "qn", "kn"):
                        PA[n] = dn.const(f"a_sb_{n}", PA[n], list(PA[n].shape))
                    qg = T(kb.sb("qg", [128, 1], F32)[:], "qg")
                    kb.ts("dve", qg, PA["qn"], 128.0 ** -0.5, None, ALU.mult)
                    PA["qg"] = qg
                    PA["kg"] = PA["kn"]
                    PA["cos_t"] = T(kb.sb("cos_t", [128, NT], F32)[:], "cos_t")
                    PA["sin_t"] = T(kb.sb("sin_t", [128, NT], F32)[:], "sin_t")
                src = hin if Lc is None else h1T
                hv = src.rearrange("(kc p) n -> p kc n", p=128)
                for t in range(TPC // NT):
                    cols = slice(t * NT, (t + 1) * NT)
                    kb.dma("sp", dn.hbuf[:], hv[:, :, cols], writes=dn.h)
                    if Lc is not None:
                        phase_C(dn, t, PC)
                    if La is not None:
                        phase_A(dn, t, PA, {"h1T": h1T, "store_ap": store_ap, "is_bf": lambda j: 8 <= j < 60})
                    else:
                        ov = hout.rearrange("(kc p) n -> p kc n", p=128)
                        kb.dma("sp", ov[:, :, cols], dn.hbuf[:], reads=dn.h)
                kb_barrier(kb)
            kb.es = kb.root_es


        def gather_x(ks):
            toks = []
            for k in ks:
                nr = XS[k + 1] - XS[k]
                CR = XCR[k]
                for c in range(nr // CR):
                    toks.append(kb.allgather(xins[k][c * CR:(c + 1) * CR, :], xouts[k][c * 4 * CR:(c + 1) * 4 * CR, :], GROUPS))
            return toks

        def relocate_x(ks):
            for k in ks:
                CR = XCR[k]
                Ck = (XS[k + 1] - XS[k]) // 4 // CR
                v = xouts[k].rearrange("(j c r i) n -> j r c (i n)", j=4, c=Ck, r=4, i=CR)
                dst = mines[k].rearrange("r G p n -> r (G p) n").rearrange("r (c i) n -> r c (i n)", i=CR)
                for r in range(4):
                    kb.dma(("sp", "act", "pool", "sp")[k], dst[r], v[jq, r])

        def mixer_phase(L):
            M = Ms[L]
            for r0 in range(0, ER, 512):
                kb.dma("sp", ein[r0:r0 + 512, 0:16], ploc[r0:r0 + 512, 0:16])
                kb.dma("sp", ein[r0:r0 + 512, 16:32], ploc[r0:r0 + 512, TPC - 16:TPC])
            kb_barrier(kb)
            kb.wait_tokens([kb.allgather(ein, eout, GROUPS)])
            kb.dma("sp", epad[ER:5 * ER, :], eout)
            kb_barrier(kb)
            kb.dma("act", eL, epad.rearrange("(b r) c -> b r c", r=ER)[jq])
            kb.dma("act", eR, epad[2 * ER:6 * ER, :].rearrange("(b r) c -> b r c", r=ER)[jq])
            kb_barrier(kb)
            t_qkv = gather_x((1, 2))
            t_g = gather_x((3,))
            t_att = gather_x((0,))
            with ExitStack() as sub:
                kb.es = sub
                kb.init_psum(2, 2, 4)
                ya = lambda g, t0, n: yin[(t0 // TPC) * YR + g * 128:(t0 // TPC) * YR + (g + 1) * 128, t0 % TPC:t0 % TPC + n]
                yr = lambda vi, t0, n: yin[(t0 // TPC) * YR + 384 + vi * 128:(t0 // TPC) * YR + 384 + (vi + 1) * 128, t0 % TPC:t0 % TPC + n]
                mixer_conv(kb, None, M["conv_w"], M["conv_b"], M["cnorm_g"], M["cnorm_b"], AC["ident"], yconv_l, xload=xload)
                kb.wait_tokens(t_qkv)
                relocate_x((1, 2))
                kb_barrier(kb)

                def mid():
                    kb.wait_tokens(t_g)
                    relocate_x((3,))
                    kb_barrier(kb)
                mixer_ret(kb, rsrc, None, None, None, M["dlog"], M["rnorm"], RC, crossb, yr, xdt=BF16, mid_hook=mid)
                t_y = []
                for c in range(4 * YR // 128):
                    if c % 5 >= 3:
                        t_y.append(kb.allgather(yin[c * 128:(c + 1) * 128, :], yout[c * 512:(c + 1) * 512, :], GROUPS))
                kb.wait_tokens(t_att)
                relocate_x((0,))
                kb_barrier(kb)
                mixer_attn(kb, qkv, None, None, M["rbT"], AC, buckets, ya, xdt=BF16)
                for c in range(4 * YR // 128):
                    if c % 5 < 3:
                        t_y.append(kb.allgather(yin[c * 128:(c + 1) * 128, :], yout[c * 512:(c + 1) * 512, :], GROUPS))
                mixer_pool(kb, None, invcnt, M["pool_w"], M["pool_scale"], ypool_l, xload=xload)
                kb.wait_tokens(t_y)
                kb_barrier(kb)
            kb.es = kb.root_es
            yv = yout.rearrange("(q c r i) n -> q r c (i n)", q=4, c=YR // 128, r=4, i=128)
            yd = ymine.rearrange("(r c i) n -> r c (i n)", r=4, c=YR // 128, i=128)
            for r in range(4):
                kb.dma(("sp", "act")[r % 2], yd[r], yv[jq, r])
            kb_barrier(kb)

        with ExitStack() as sub:
            kb.es = sub
            z = T(kb.sb("zeros", [128, ER // 128, 32], F32)[:], "zeros")
            kb.op("dve", lambda: nc.vector.memset(z.ap, 0.0), writes=[z])
            kb.dma("sp", epad[0:ER, :].rearrange("(k p) c -> p k c", p=128), z.ap, reads=[z])
            kb.dma("sp", epad[5 * ER:6 * ER, :].rearrange("(k p) c -> p k c", p=128), z.ap, reads=[z])
            kb_barrier(kb)
        kb.es = kb.root_es
        if debug == "xchg":
            seed = di("seed", [128, TPC])
            dbg = nc.dram_tensor("dbg", [6, 128, TPC], F32, kind="ExternalOutput").ap()
            for k in range(3, 4):
                nr = XS[k + 1] - XS[k]
                for r0 in range(0, nr, 128):
                    kb.dma("sp", xins[k][r0:r0 + 128, :], seed)
            for r0 in range(0, ER, 128):
                kb.dma("sp", ploc[r0:r0 + 128, :], seed)
            kb_barrier(kb)
            gather_x((3,)); kb.wait_cc(); relocate_x((3,)); kb_barrier(kb)
            kb.dma("sp", dbg[0], mines[3][1, 0, 0:128, :])
            kb.dma("sp", dbg[1], mines[3][3, 0, 128:256, :])
            kb.dma("sp", dbg[2], mines[3][2, 0, 128:256, :])
            kb.dma("sp", dbg[3, :, 0:32], eL[256:384, :])
            kb.dma("sp", dbg[4, :, 0:32], eR[256:384, :])
            kb.dma("sp", dbg[5], xouts[3][0:128, :])
            kb.finish("sp")
            return nc
        if debug == "mix":
            dbg = nc.dram_tensor("dbg", [128, 128], F32, kind="ExternalOutput").ap()
            mixer_phase(0)
            kb.dma("sp", dbg, ploc[0:128, 0:128])
            kb.finish("sp")
            return nc
        dense_phase(None, 0)
        for L in range(DEPTH):
            mixer_phase(L)
            dense_phase(L, L + 1 if L + 1 < DEPTH else None)
        kb.finish("sp")
    return nc


def col_layout(v):
    return np.ascontiguousarray(np.asarray(v, np.float32).reshape(-1, 128).T)


def rope_tables(core):
    pos = ((core % 4) * TPC + np.arange(TPC)).astype(np.float32)
    inv = (10000.0 ** (-np.linspace(0.0, 1.0, 64, dtype=np.float32))).astype(np.float32)
    ang = pos[:, None] * inv[None, :]
    cos = np.cos(ang).astype(np.float32).T
    sin = np.sin(ang).astype(np.float32).T
    return (np.ascontiguousarray(np.concatenate([cos, cos], 0)), np.ascontiguousarray(np.concatenate([-sin, sin], 0)))


def swap_cols(w_in):
    blk = w_in[:, 5632:6656].reshape(w_in.shape[0], 8, 2, 64)
    return np.ascontiguousarray(blk[:, :, ::-1, :].reshape(w_in.shape[0], 1024))


def halo_slice(fm, qt, H):
    s0 = qt * TPC
    out = np.zeros((fm.shape[0], TPC + 2 * H), fm.dtype)
    lo = max(s0 - H, 0)
    hi = min(s0 + TPC + H, SEQ)
    out[:, lo - (s0 - H):hi - (s0 - H)] = fm[:, lo:hi]
    return out


def a_inputs(I, L):
    w_in = I["w_in"][L]
    base = {"a_ffn1_g": col_layout(I["ffn1_norm"][L]), "a_mix_g": col_layout(I["mix_norm"][L]),
            "a_qn": col_layout(I["q_norm"][L]), "a_kn": col_layout(I["k_norm"][L]),
            "a_ffn1_wg": I["ffn1_w_gate"][L], "a_ffn1_wu": I["ffn1_w_up"][L], "a_ffn1_wd": I["ffn1_w_down"][L],
            "a_w_in": w_in, "a_w_in_sw": swap_cols(w_in)}
    per = []
    for c in range(NCORES):
        cosT, sinT = rope_tables(c)
        per.append({"a_cosT": cosT, "a_sinT": sinT})
    return base, per


def c_inputs(I, L):
    base = {"c_mix_g": col_layout(I["mix_norm"][L]), "c_ffn2_g": col_layout(I["ffn2_norm"][L]), "c_ple_g": col_layout(I["ple_norm"][L]),
            "c_b_gate": col_layout(I["b_gate"][L]), "c_w_gate": I["w_gate"][L], "c_w_br_pool": I["w_br_pool"][L],
            "c_w_br_attn": I["w_br_attn"][L], "c_w_br_ret": I["w_br_ret"][L], "c_w_br_conv": I["w_br_conv"][L],
            "c_w_out": I["w_out"][L], "c_ffn2_wg": I["ffn2_w_gate"][L], "c_ffn2_wu": I["ffn2_w_up"][L],
            "c_ffn2_wd": I["ffn2_w_down"][L], "c_w_ple_gate": I["w_ple_gate"][L], "c_w_ple_proj": I["w_ple_proj"][L]}
    per = []
    for c in range(NCORES):
        b, qt = divmod(c, 4)
        per.append({"c_pT": np.ascontiguousarray(I["p"][L, b, qt * TPC:(qt + 1) * TPC].T)})
    return base, per


def mixer_inputs(I, L, projT):
    rc = ret_consts()
    ac, _ = att_consts()
    consts = {"rc_" + k: v for k, v in rc.items()}
    consts.update({"ac_" + k: v for k, v in ac.items()})
    cw = np.asarray(I["conv_w"][L])
    base = {"pool_w": np.ascontiguousarray(np.asarray(I["pool_w"][L]).reshape(1024, 256)), "pool_scale": col_layout(I["pool_scale"][L]),
            "conv_w": np.ascontiguousarray(cw.T.reshape(8, 128, 31).transpose(1, 0, 2)), "conv_b": col_layout(I["conv_b"][L]),
            "cnorm_g": col_layout(I["conv_norm_g"][L]), "cnorm_b": col_layout(I["conv_norm_b"][L])}
    base.update(consts)
    rbT = np.ascontiguousarray(np.asarray(I["rel_bias"]).T)
    per = []
    for c in range(NCORES):
        b, j = divmod(c, 4)
        pb = np.concatenate([projT[b * 4 + q] for q in range(4)], axis=1)
        hs = [j, 4 + j, 8 + j]
        t = (j * TPC + np.arange(TPC))[None, :]
        w = np.array(POOL_WIN)[:, None]
        lo = np.clip(t - w // 2, 0, SEQ)
        hi = np.clip(t - w // 2 + w, 0, SEQ)
        m = {"q3": np.stack([pb[1024 + h * 128:1024 + (h + 1) * 128] for h in hs]),
             "k3": np.stack([pb[2560 + h * 128:2560 + (h + 1) * 128] for h in hs]),
             "v3": np.stack([pb[4096 + h * 128:4096 + (h + 1) * 128] for h in hs]),
             "rbT": np.ascontiguousarray(rbT[hs]),
             "qT": np.ascontiguousarray(pb[5632 + j * 128:5632 + (j + 1) * 128]),
             "kT": np.ascontiguousarray(pb[6144 + j * 128:6144 + (j + 1) * 128]),
             "vT": np.ascontiguousarray(pb[6656 + j * 256:6656 + (j + 1) * 256]),
             "gT": np.ascontiguousarray(pb[7680 + j * 256:7680 + (j + 1) * 256]),
             "dlog": np.ascontiguousarray(np.asarray(I["ret_decay_logit"][L])[:, j].reshape(1, 2)),
             "rnorm": col_layout(np.asarray(I["ret_norm"][L])[j * 256:(j + 1) * 256]),
             "xp_h": halo_slice(pb[0:1024], j, PH), "invcnt": np.full((4, TPC), 1.0, np.float32),
             "cin_h": halo_slice(pb[8704:10752], j, CH)}
        m["invcnt"] = INVCNT[j]
        per.append(m)
        del pb
    return base, per


def _invcnt_tables():
    out = []
    for qt in range(4):
        t = (qt * TPC + np.arange(TPC))[None, :]
        w = np.array(POOL_WIN)[:, None]
        lo = np.clip(t - w // 2, 0, SEQ)
        hi = np.clip(t - w // 2 + w, 0, SEQ)
        cnt = (hi - lo).astype(np.float32)
        tab = np.empty((4, TPC), np.float32)
        for g in range(4):
            tab[g] = np.array([1.0 / c for c in cnt[g]], np.float32)
        out.append(tab)
    return out


INVCNT = _invcnt_tables()
_PROG = []


def kernel(**inputs):
    I = {k: np.asarray(v) for k, v in inputs.items()}
    x = I["x"]
    if not _PROG:
        _PROG.append(build_fused())
    base = {}
    per = [dict() for _ in range(NCORES)]
    rbT = np.ascontiguousarray(I["rel_bias"].T)
    for L in range(DEPTH):
        ab, ap_ = a_inputs(I, L)
        cb, cp_ = c_inputs(I, L)
        base.update({f"a{L}_" + k[2:]: v for k, v in ab.items()})
        base.update({f"c{L}_" + k[2:]: v for k, v in cb.items()})
        cw = I["conv_w"][L]
        base.update({f"m{L}_pool_w": np.ascontiguousarray(I["pool_w"][L].reshape(1024, 256)), f"m{L}_pool_scale": col_layout(I["pool_scale"][L]),
                     f"m{L}_conv_w": np.ascontiguousarray(cw.T.reshape(8, 128, 31).transpose(1, 0, 2)), f"m{L}_conv_b": col_layout(I["conv_b"][L]),
                     f"m{L}_cnorm_g": col_layout(I["conv_norm_g"][L]), f"m{L}_cnorm_b": col_layout(I["conv_norm_b"][L])})
        for c in range(NCORES):
            j = c % 4
            per[c].update({f"a{L}_" + k[2:]: v for k, v in ap_[c].items()})
            per[c].update({f"c{L}_" + k[2:]: v for k, v in cp_[c].items()})
            per[c][f"m{L}_rbT"] = np.ascontiguousarray(rbT[[j, 4 + j, 8 + j]])
            per[c][f"m{L}_dlog"] = np.ascontiguousarray(I["ret_decay_logit"][L][:, j].reshape(1, 2))
            per[c][f"m{L}_rnorm"] = col_layout(I["ret_norm"][L][j * 256:(j + 1) * 256])
    base.update({"rc_" + k: v for k, v in ret_consts().items()})
    base.update({"ac_" + k: v for k, v in att_consts()[0].items()})
    in_maps = []
    for c in range(NCORES):
        m = dict(base)
        m.update(per[c])
        m["invcnt"] = INVCNT[c % 4]
        m["hin"] = np.ascontiguousarray(x[c // 4, (c % 4) * TPC:(c % 4 + 1) * TPC].T)
        in_maps.append(m)
    res = run_bass_kernel_spmd(_PROG[0], in_maps, core_ids=list(range(NCORES)))
    out = np.empty(x.shape, np.float32)
    for c in range(NCORES):
        out[c // 4, (c % 4) * TPC:(c % 4 + 1) * TPC] = res.results[c]["hout"].T
    return out
```
